# Optimizing a Trainium2 kernel written in Bass

```python
import math
import jax, jax.numpy as jnp
from jax import lax
import numpy as np

D_MODEL = 1024
BATCH = 8
SEQ = 4096
DEPTH = 1

CHUNK = 64
Q_BLOCK = 128
A_HEADS = 8
A_HEAD_DIM = 64
A_WIDTH = A_HEADS * A_HEAD_DIM
IDX_HEADS = 8
IDX_DIM = 64
TOPK_MAX = 256
REL_BUCKETS = 32
REL_MAX_DIST = 128
B_WIDTH = 512
CONV_K = 31
PLE_DIM = 256
EPS = 1e-6
IN_SIZES = (A_WIDTH, A_WIDTH, A_WIDTH, A_WIDTH,
            IDX_HEADS * IDX_DIM, IDX_DIM, IDX_HEADS,
            2 * B_WIDTH, B_WIDTH,
            D_MODEL, D_MODEL)
IN_WIDTH = 4 * A_WIDTH + IDX_HEADS * IDX_DIM + IDX_DIM + IDX_HEADS + 3 * B_WIDTH + 2 * D_MODEL

kernel_name = "hybrid_dsa_conformer_gated_block"


def rms_norm(x, g):
    xf = x.astype(jnp.float32)
    y = xf * lax.rsqrt(jnp.mean(xf * xf, axis=-1, keepdims=True) + EPS)
    return (y * g.astype(jnp.float32)).astype(x.dtype)


def layer_norm(x, g, b):
    xf = x.astype(jnp.float32)
    mu = jnp.mean(xf, axis=-1, keepdims=True)
    xc = xf - mu
    var = jnp.mean(xc * xc, axis=-1, keepdims=True)
    y = xc * lax.rsqrt(var + EPS) * g.astype(jnp.float32) + b.astype(jnp.float32)
    return y.astype(x.dtype)


def t5_bucket(rel):
    half = REL_BUCKETS // 2
    max_exact = half // 2
    base = jnp.where(rel > 0, half, 0).astype(jnp.int32)
    n = jnp.abs(rel)
    nf = jnp.maximum(n, 1).astype(jnp.float32)
    large = max_exact + (jnp.log(nf / max_exact) / math.log(REL_MAX_DIST / max_exact)
                         * (half - max_exact)).astype(jnp.int32)
    large = jnp.minimum(large, half - 1)
    return base + jnp.where(n < max_exact, n, large)


def sparse_attention(q, k, v, q_idx, k_idx, w_idx, rel_table):
    bsz, seq = q.shape[0], q.shape[1]
    k_top = min(TOPK_MAX, seq // 4)
    n_blocks = seq // Q_BLOCK
    pos = jnp.arange(seq, dtype=jnp.int32)
    key_chunk = pos // CHUNK
    idx_scale = (IDX_DIM ** -0.5) * (IDX_HEADS ** -0.5)
    attn_scale = A_HEAD_DIM ** -0.5
    gather = jax.vmap(lambda a, i: a[i])

    def to_blocks(a):
        return a.reshape(bsz, n_blocks, Q_BLOCK, *a.shape[2:]).swapaxes(0, 1)

    def block_fn(args):
        qb, qib, wib, qpos = args
        dots = jnp.einsum('bqhd,bkd->bqhk', qib, k_idx).astype(jnp.float32)
        scores = jnp.einsum('bqhk,bqh->bqk', jax.nn.relu(dots),
                            wib.astype(jnp.float32) * idx_scale)
        admissible = key_chunk[None, :] <= (qpos // CHUNK)[:, None]
        scores = jnp.where(admissible[None], scores, -jnp.inf)
        top_val, top_idx = lax.top_k(scores, k_top)
        valid = jnp.isfinite(top_val)
        k_sel = gather(k, top_idx)
        v_sel = gather(v, top_idx)
        logits = jnp.einsum('bqhd,bqkhd->bqhk', qb, k_sel).astype(jnp.float32) * attn_scale
        rel = top_idx - qpos[None, :, None]
        bias = rel_table.astype(jnp.float32)[t5_bucket(rel)]
        logits = logits + jnp.moveaxis(bias, -1, 2)
        logits = jnp.where(valid[:, :, None, :], logits, -jnp.inf)
        probs = jax.nn.softmax(logits, axis=-1).astype(v.dtype)
        return jnp.einsum('bqhk,bqkhd->bqhd', probs, v_sel)

    out = lax.map(block_fn, (to_blocks(q), to_blocks(q_idx), to_blocks(w_idx),
                             pos.reshape(n_blocks, Q_BLOCK)))
    return out.swapaxes(0, 1).reshape(bsz, seq, A_HEADS, A_HEAD_DIM)


def causal_depthwise_conv(u, w, b):
    y = lax.conv_general_dilated(u, w.astype(u.dtype), window_strides=(1,),
                                 padding=[(CONV_K - 1, 0)],
                                 dimension_numbers=('NWC', 'WIO', 'NWC'),
                                 feature_group_count=u.shape[-1])
    return y + b.astype(u.dtype)


def setup_inputs(seed: int = 0) -> dict:
    key = jax.random.key(seed)
    ks = jax.random.split(key, 18)
    f32 = jnp.float32
    nrm = lambda k, shape, scale: jax.random.normal(k, shape, f32) * scale
    return {
        "x": nrm(ks[0], (BATCH, SEQ, D_MODEL), 1.0),
        "p": nrm(ks[1], (DEPTH, BATCH, SEQ, PLE_DIM), 1.0),
        "norm_in_g": 1.0 + nrm(ks[2], (DEPTH, D_MODEL), 0.02),
        "w_in": nrm(ks[3], (DEPTH, D_MODEL, IN_WIDTH), D_MODEL ** -0.5),
        "conv_w": nrm(ks[4], (DEPTH, CONV_K, 1, B_WIDTH), CONV_K ** -0.5),
        "conv_b": nrm(ks[5], (DEPTH, B_WIDTH), 0.02),
        "conv_ln_g": 1.0 + nrm(ks[6], (DEPTH, B_WIDTH), 0.02),
        "conv_ln_b": nrm(ks[7], (DEPTH, B_WIDTH), 0.02),
        "w_branch_a": nrm(ks[8], (DEPTH, A_WIDTH, D_MODEL), A_WIDTH ** -0.5),
        "w_branch_b": nrm(ks[9], (DEPTH, B_WIDTH, D_MODEL), B_WIDTH ** -0.5),
        "w_out": nrm(ks[10], (DEPTH, D_MODEL, D_MODEL), D_MODEL ** -0.5),
        "ple_norm_g": 1.0 + nrm(ks[11], (DEPTH, D_MODEL), 0.02),
        "w_ple_gate": nrm(ks[12], (DEPTH, D_MODEL, D_MODEL), D_MODEL ** -0.5),
        "w_ple_proj": nrm(ks[13], (DEPTH, PLE_DIM, D_MODEL), PLE_DIM ** -0.5),
        "rel_bias": nrm(ks[14], (REL_BUCKETS, A_HEADS), 0.5),
        "final_norm_g": 1.0 + nrm(ks[15], (D_MODEL,), 0.02),
    }


def reference(x, p, norm_in_g, w_in, conv_w, conv_b, conv_ln_g, conv_ln_b,
              w_branch_a, w_branch_b, w_out, ple_norm_g, w_ple_gate, w_ple_proj,
              rel_bias, final_norm_g):
    bsz, seq, _ = x.shape
    offsets = list(np.cumsum(IN_SIZES)[:-1])
    for i in range(DEPTH):
        h = rms_norm(x, norm_in_g[i])
        proj = h @ w_in[i]
        (q, k, v, z_a, q_idx, k_idx, w_idx, glu_in, z_b,
         gate_a, gate_b) = jnp.split(proj, offsets, axis=-1)
        heads = lambda t: t.reshape(bsz, seq, A_HEADS, A_HEAD_DIM)
        attn = sparse_attention(heads(q), heads(k), heads(v),
                                q_idx.reshape(bsz, seq, IDX_HEADS, IDX_DIM),
                                k_idx, w_idx, rel_bias)
        y_a = (attn.reshape(bsz, seq, A_WIDTH) * jax.nn.silu(z_a)) @ w_branch_a[i]
        u = glu_in[..., :B_WIDTH] * jax.nn.sigmoid(glu_in[..., B_WIDTH:])
        c = causal_depthwise_conv(u, conv_w[i], conv_b[i])
        c = jax.nn.silu(layer_norm(c, conv_ln_g[i], conv_ln_b[i]))
        y_b = (c * jax.nn.silu(z_b)) @ w_branch_b[i]
        merged = jax.nn.sigmoid(gate_a) * y_a + jax.nn.sigmoid(gate_b) * y_b
        x = x + merged @ w_out[i]
        e = p[i] @ w_ple_proj[i]
        g = jax.nn.sigmoid(rms_norm(x, ple_norm_g[i]) @ w_ple_gate[i])
        x = x + g * e
    return rms_norm(x, final_norm_g)
```

```python
import math
import os
from contextlib import ExitStack

import numpy as np
import concourse.bass as bass
import concourse.mybir as mybir
from concourse.bass_utils import run_bass_kernel_spmd

F32 = mybir.dt.float32
BF16 = mybir.dt.bfloat16
U8 = mybir.dt.uint8
AF = mybir.ActivationFunctionType
ALU = mybir.AluOpType
AX = mybir.AxisListType

D = 1024
KC = 8
G = 256
SUB = G // 128
NIT = 16
TOPK = 256
PLE = 256
CONVK = 31
IDX_SCALE = (64 ** -0.5) * (8 ** -0.5)
EPS = 1e-6
NEG = -1.0e30

SEG = dict(q=0, k=512, v=1024, za=1536, qi=2048, ki=2560, wi=2624, ga_=2632, gb_=3144,
           zb=3656, gta=4168, gtb=5192)
WBLOCKS = ["q", "k", "v", "za", "qi", "kiwi", "glua", "glub", "zb", "gta0", "gta1", "gtb0", "gtb1"]
WB_COL = dict(q=0, k=512, v=1024, za=1536, qi=2048, glua=2632, glub=3144, zb=3656,
              gta0=4168, gta1=4680, gtb0=5192, gtb1=5704)

COMPUTE = ("pe", "act", "dve", "pool")


class _Op:
    __slots__ = ("eng", "fn", "reads", "writes", "dma", "deps", "signal", "sig_idx",
                 "dsem", "dval", "idx")

    def __init__(self, eng, fn, reads, writes, dma):
        self.eng = eng
        self.fn = fn
        self.reads = tuple(reads)
        self.writes = tuple(writes)
        self.dma = dma
        self.deps = ()
        self.signal = False
        self.sig_idx = 0
        self.dsem = None
        self.dval = 0


class Sched:
    def __init__(self, nc):
        self.nc = nc
        self.ops = []

    enabled = True

    def add(self, eng, fn, reads=(), writes=(), dma=False):
        import os
        lim = int(os.environ.get("MK_NOPS", "0"))
        if self.enabled and (lim == 0 or len(self.ops) < lim):
            self.ops.append(_Op(eng, fn, reads, writes, dma))

    def pe(self, fn, reads=(), writes=()):
        self.add("pe", fn, reads, writes)

    def act(self, fn, reads=(), writes=()):
        self.add("act", fn, reads, writes)

    def dve(self, fn, reads=(), writes=()):
        self.add("dve", fn, reads, writes)

    def pool(self, fn, reads=(), writes=()):
        self.add("pool", fn, reads, writes)

    def dma(self, q, fn, reads=(), writes=()):
        self.add(q, fn, reads, writes, dma=True)

    def _analyze(self):
        ops = self.ops
        W = {}
        R = {}
        for idx, op in enumerate(ops):
            op.idx = idx
            deps = set()
            for r in op.reads:
                w = W.get(r)
                if w:
                    deps.update(w[0].values())
                    deps.update(w[1])
            for wkey in op.writes:
                rd = R.get(wkey)
                w = W.get(wkey)
                if rd and (rd[0] or rd[1]):
                    deps.update(rd[0].values())
                    deps.update(rd[1])
                    if w:
                        deps.update(w[0].values())
                        deps.update(w[1])
                    W[wkey] = ({}, [])
                    R[wkey] = ({}, [])
                else:
                    if w:
                        deps.update(w[0].values())
                        deps.update(w[1])
                w = W.setdefault(wkey, ({}, []))
                if op.dma:
                    w[1].append(idx)
                else:
                    w[0][op.eng] = idx
            for r in op.reads:
                rd = R.setdefault(r, ({}, []))
                if op.dma:
                    rd[1].append(idx)
                else:
                    rd[0][op.eng] = idx
            deps.discard(idx)
            best = {}
            dmas = []
            for d in deps:
                p = ops[d]
                if p.dma:
                    dmas.append(d)
                else:
                    if p.eng == op.eng and not op.dma and op.eng == "pe":
                        continue
                    if p.eng not in best or best[p.eng] < d:
                        best[p.eng] = d
            op.deps = tuple(best.values()) + tuple(dmas)
            for d in best.values():
                ops[d].signal = True
        last = {}
        for op in ops:
            if not op.dma:
                last[op.eng] = op
        for op in last.values():
            op.signal = True

    def emit(self, sems, dma_sems):
        nc = self.nc
        self._analyze()
        engobj = {"pe": nc.tensor, "act": nc.scalar, "dve": nc.vector, "pool": nc.gpsimd,
                  "sp": nc.sync}
        cnt = {e: 0 for e in COMPUTE}
        dcount = {q: 0 for q in dma_sems}
        waited = {}
        nwaits = 0
        for op in self.ops:
            e = engobj[op.eng]
            need = {}
            for d in op.deps:
                p = self.ops[d]
                if p.dma:
                    key = ("d", id(p.dsem))
                    sem, val = p.dsem, p.dval
                else:
                    key = ("c", p.eng)
                    sem, val = sems[p.eng], p.sig_idx
                if key not in need or need[key][1] < val:
                    need[key] = (sem, val)
            if op.dma:
                k = dcount[op.eng]
                dcount[op.eng] += 1
                pool = dma_sems[op.eng]
                op.dsem = pool[k % len(pool)]
                op.dval = 16 * (k // len(pool) + 1)
                if op.dval > 16:
                    key = ("d", id(op.dsem))
                    if key not in need or need[key][1] < op.dval - 16:
                        need[key] = (op.dsem, op.dval - 16)
            if os.environ.get("MK_DUMP"):
                print("OP", op.idx, op.eng, "dma" if op.dma else "", "R", op.reads, "W", op.writes, "deps",
                      [(self.ops[d].eng, d) for d in op.deps], "need", [(k[0], getattr(s_, "name", None), v) for k, (s_, v) in need.items()])
            for key, (sem, val) in need.items():
                wk = (op.eng, key)
                if waited.get(wk, 0) >= val:
                    continue
                waited[wk] = val
                e.wait_ge(sem, val)
                nwaits += 1
            ins = op.fn(e)
            if op.dma:
                ins.then_inc(op.dsem, 16)
            elif op.signal:
                cnt[op.eng] += 1
                op.sig_idx = cnt[op.eng]
                ins.then_inc(sems[op.eng], 1)
        for eng_ in COMPUTE:
            if cnt[eng_] > 0:
                nc.sync.wait_ge(sems[eng_], cnt[eng_])
        for q, pool in dma_sems.items():
            k = dcount[q]
            for i, s in enumerate(pool):
                uses = (k - i + len(pool) - 1) // len(pool) if k > i else 0
                if uses > 0:
                    nc.sync.wait_ge(s, 16 * uses)
        return dict(n_ops=len(self.ops), n_waits=nwaits, sig=cnt, dma=dcount)


def build(S):
    NG = S // G
    NT = S // 128
    nc = bass.Bass("TRN2", target_bir_lowering=False)

    def din(name, shape, dt=F32):
        return nc.dram_tensor(name, list(shape), dt, kind="ExternalInput").ap()

    x_d = din("x", [S, D])
    p_d = din("p", [S, PLE])
    w_in_d = din("w_in", [D, 6216])
    w_a_d = din("w_a", [512, D])
    w_b_d = din("w_b", [512, D])
    w_out_d = din("w_out", [D, D])
    w_gate_d = din("w_gate", [D, D])
    w_proj_d = din("w_proj", [PLE, D])
    gin_d = din("gin", [128, KC])
    gple_d = din("gple", [128, KC])
    gfin_d = din("gfin", [128, D])
    convw_d = din("convw", [128, 4, CONVK])
    convb_d = din("convb", [128, 4])
    lng_d = din("lng", [128, 4])
    lnb_d = din("lnb", [128, 4])
    d0t_d = din("d0t", [128, 8, 128])
    d1t_d = din("d1t", [128, 8, 128])
    b15_d = din("b15", [128, 8])
    ident_d = din("ident", [128, 128])
    pow2_d = din("pow2", [128, NIT + 2])
    out_d = nc.dram_tensor("out", [S, D], F32, kind="ExternalOutput").ap()

    wq_in = nc.dram_tensor("wq_in", [len(WBLOCKS), 128, 4096], BF16, kind="Internal").ap()
    wq_a = nc.dram_tensor("wq_a", [2, 128, 2048], BF16, kind="Internal").ap()
    wq_b = nc.dram_tensor("wq_b", [2, 128, 2048], BF16, kind="Internal").ap()
    wq_out = nc.dram_tensor("wq_out", [2, 128, 4096], BF16, kind="Internal").ap()
    wq_gate = nc.dram_tensor("wq_gate", [2, 128, 4096], BF16, kind="Internal").ap()
    wq_proj = nc.dram_tensor("wq_proj", [128, 2048], BF16, kind="Internal").ap()

    sc = Sched(nc)
    st = ExitStack()
    import os
    stop_at = os.environ.get("MK_STOP", "")

    def chk(tag):
        if stop_at and tag == stop_at:
            sc.enabled = False

    dbg_name = os.environ.get("MK_DBG", "")

    def dbg(name, ap_fn, keys, ncols):
        if dbg_name != name or not sc.enabled:
            return
        sc.dve(lambda e: e.tensor_copy(out=dbgt[:, 0:ncols], in_=ap_fn()), reads=keys, writes=["dbgt"])
        sc.dma("sp", lambda e: e.dma_start(out=out_d[0:128, 0:ncols], in_=dbgt[:, 0:ncols]), reads=["dbgt"])
        sc.enabled = False

    def sb(name, shape, dt):
        return st.enter_context(nc.sbuf_tensor("sb_" + name, list(shape), dt))

    kT = sb("kT", [128, 4, S], BF16)
    vaug = sb("vaug", [128, NT, 4, 160], BF16)
    kidx2 = sb("kidx2", [128, S], BF16)
    identb = sb("identb", [128, 128], BF16)
    d0t = sb("d0t", [128, 8, 128], F32)
    d1t = sb("d1t", [128, 8, 128], F32)
    b15 = sb("b15", [128, 8], F32)
    pow2 = sb("pow2", [128, NIT + 2], F32)
    gin = sb("gin", [128, KC], F32)
    gple = sb("gple", [128, KC], F32)
    gfin = sb("gfin", [128, D], F32)
    convw = sb("convw", [128, 4, CONVK], F32)
    convb = sb("convb", [128, 4], F32)
    lng = sb("lng", [128, 4], F32)
    lnb = sb("lnb", [128, 4], F32)

    big = sb("big", [128, 4096], F32)
    ptl = sb("ptl", [128, SUB, PLE], F32)
    pbf = sb("pbf", [128, SUB, PLE], BF16)
    pT = sb("pT", [128, 2, G], BF16)
    hT2 = [sb("hT0", [128, KC, G], BF16), sb("hT1", [128, KC, G], BF16)]
    wblk = [sb(f"wblk{i}", [128, 4096], BF16) for i in range(3)]
    wbr = sb("wbr", [128, 2048], BF16)
    qTe = sb("qTe", [128, 4, G], BF16)
    qTo = sb("qTo", [128, 4, G], BF16)
    szaT = sb("szaT", [128, 4, G], BF16)
    qidxTe = sb("qidxTe", [128, 4, G], BF16)
    qidxTo = sb("qidxTo", [128, 4, G], BF16)
    szbT = sb("szbT", [128, 4, G], BF16)
    uT = sb("uT", [128, 4, 30 + G], F32)
    gtmp = [sb(f"gtmp{i}", [128, G], BF16) for i in range(2)]
    wabs = sb("wabs", [128, SUB, 8], F32)
    wsgn = sb("wsgn", [128, SUB, 8], F32)
    stat = sb("stat", [128, 32], F32)
    Mq = sb("Mq", [128, 4096], BF16)
    MT = sb("MT", [128, NT, G], U8)
    xn = Mq[:, 0:D]
    PT = [sb(f"PT{i}", [128, G], BF16) for i in range(5)]
    tmpD = [sb(f"tmpD{i}", [128, 128], F32) for i in range(2)]
    hl = [[PT[0], PT[1]], [PT[2], PT[3]]]
    gatedT = sb("gatedT", [128, 4, G], BF16)
    gatedbT = sb("gatedbT", [128, 4, G], BF16)
    cacc = sb("cacc", [128, 4, G], F32)
    lnt_all = sb("lnt_all", [128, 5, G], F32)
    lnt = [lnt_all[:, i, :] for i in range(5)]
    Rb = [lnt_all[:, 0:2, :].rearrange("p a b -> p (a b)"), lnt_all[:, 2:4, :].rearrange("p a b -> p (a b)")]
    rden = sb("rden", [128, G], F32)
    neghalf = sb("neghalf", [128, 8], F32)
    sel_e = sb("sel_e", [128, 128], F32)
    sel_o = sb("sel_o", [128, 128], F32)
    oneslnb = sb("oneslnb", [128, 128], BF16)
    bcs = lnt[1]
    t1 = lnt[2]
    mab = [sb(f"mab{i}", [128, G], F32) for i in range(2)]
    mergedT = sb("mergedT", [128, KC, G], BF16)
    sqb = mergedT[:].rearrange("p a b -> p (a b)").bitcast(F32).rearrange("p (c t) -> p c t", c=4)
    bis = sb("bis", [128, NIT + 8], F32)
    dbgt = sb("dbgt", [128, 1024], F32) if os.environ.get("MK_DBG") else None
    prb = sb("prb", [128, 12], F32)
    nbis = sb("nbis", [128, NIT + 8], F32)

    ps = [st.enter_context(nc.psum_tensor(f"ps{i}", [128, 512], F32)) for i in range(8)]
    sems = {e: st.enter_context(nc.semaphore(f"s_{e}")) for e in COMPUTE}
    dsems = {q: [st.enter_context(nc.semaphore(f"d_{q}{i}")) for i in range(16)]
             for q in ("sp", "pool")}

    xs = big[:, 0:SUB * D].rearrange("p (s d) -> p s d", s=SUB)
    acc = big
    gsb = big[:, 2048:2560]
    tmpe = big[:, 2560:3072]
    ob = big[:, 3072:4096]

    def bigkeys(c0, c1):
        return [f"big{k}" for k in range(c0 // 512, (c1 + 511) // 512)]

    rot = {"w": 0, "a": 0}

    def ps_full(kind="w"):
        base = 4 if kind == "w" else 0
        rot[kind] = (rot[kind] + 1) & ~1
        b = base + (rot[kind] // 2) % 4
        rot[kind] += 2
        if os.environ.get("MK_PSDBG"):
            import traceback
            fr = traceback.extract_stack(limit=3)
            print("PSALLOC", b, len(sc.ops), [f.lineno for f in fr[:-1]])
        return ps[b], [f"ps{b}a", f"ps{b}b"]

    att_rot = [0]

    def ps_att():
        b = 2 + att_rot[0] % 6
        att_rot[0] += 1
        return ps[b][:, 0:256], [f"ps{b}a", f"ps{b}b"]

    half_owner = {}

    def ps_half(kind="w"):
        t, keys = ps_full(kind)
        ap = t[:, 0:256]
        half_owner[id(ap)] = t
        return ap, keys

    def pe_fence(pst, keys):
        sc.pe(lambda e: e.matmul(pst[:, 448:449], lhsT=identb[:], rhs=identb[:, 0:1], start=True, stop=True),
              reads=["identb"], writes=keys)

    cp_rr = [int(os.environ.get("MK_CP", "0"))]

    def copy_any(out, in_, reads, writes):
        cp_rr[0] ^= 1
        if cp_rr[0]:
            sc.act(lambda e: e.activation(out=out, in_=in_, func=AF.Identity), reads, writes)
        else:
            sc.dve(lambda e: e.tensor_copy(out=out, in_=in_), reads, writes)

    def ld(q, dst, src, key):
        sc.dma(q, lambda e: e.dma_start(out=dst, in_=src), writes=[key])

    ld("sp", big[:, 0:128], ident_d, "big0")
    sc.act(lambda e: e.activation(out=identb[:], in_=big[:, 0:128], func=AF.Identity),
           reads=["big0"], writes=["identb"])
    ld("sp", d0t[:], d0t_d, "d0t")
    ld("sp", d1t[:], d1t_d, "d1t")
    ld("sp", b15[:], b15_d, "b15")
    ld("sp", pow2[:], pow2_d, "pow2")
    ld("sp", gin[:], gin_d, "gin")
    ld("sp", gple[:], gple_d, "gple")
    ld("sp", gfin[:], gfin_d, "gfin")
    ld("sp", convw[:], convw_d, "convw")
    ld("sp", convb[:], convb_d, "convb")
    ld("sp", lng[:], lng_d, "lng")
    ld("sp", lnb[:], lnb_d, "lnb")
    sc.pool(lambda e: e.memset(neghalf[:], -0.5), writes=["neghalf"])
    sc.pool(lambda e: e.memset(qTe[:], 0.0), writes=["qTe"])
    sc.pool(lambda e: e.memset(qTo[:], 0.0), writes=["qTo"])
    sc.pool(lambda e: e.memset(qidxTe[:], 0.0), writes=["qidxTe"])
    sc.pool(lambda e: e.memset(qidxTo[:], 0.0), writes=["qidxTo"])
    sc.pool(lambda e: e.memset(oneslnb[:], 1.0 / 512.0), writes=["oneslnb"])
    sc.pool(lambda e: e.memset(vaug[:, :, :, 64:96], 0.0), writes=["vaug_c"])
    sc.pool(lambda e: e.memset(vaug[:, :, :, 64:65], 1.0), writes=["vaug_c"])
    sc.pool(lambda e: e.memset(uT[:, :, 0:30], 0.0), writes=["uTh"])
    sc.pool(lambda e: e.memset(rden[:], 0.0), writes=["rden0", "rden1"])
    sc.pool(lambda e: e.memset(sel_e[:], 0.0), writes=["sel"])
    sc.pool(lambda e: e.memset(sel_o[:], 0.0), writes=["sel"])
    sc.pool(lambda e: e.memset(sel_e[64:65, :], 1.0), writes=["sel"])
    sc.pool(lambda e: e.memset(sel_o[32:33, :], 1.0), writes=["sel"])

    chk('w')
    w_in_v = w_in_d.rearrange("(k p) c -> p k c", p=128)
    stg = [0]

    def stage_weight(parts, dst):
        i = stg[0] % 3
        stg[0] += 1
        for (dv, srcap) in parts:
            sc.dma("pool", lambda e, dv=dv, srcap=srcap, i=i: e.dma_start(out=dv(wblk[i]), in_=srcap),
                   writes=[f"wblk{i}"])
        n = dst.shape[-1]
        sc.dma("sp", lambda e, i=i, dst=dst, n=n: e.dma_start(out=dst, in_=wblk[i][:, 0:n]),
               reads=[f"wblk{i}"], writes=["wq"])

    v8 = lambda t: t[:].rearrange("p (k c) -> p k c", k=8)
    v4 = lambda t: t[:].rearrange("p (k c) -> p k c", k=4)
    v2 = lambda t: t[:, 0:2048].rearrange("p (k c) -> p k c", k=2)
    for bi, name in enumerate(WBLOCKS):
        if name == "kiwi":
            parts = [
                (lambda t: v8(t)[:, :, 0:64], w_in_v[:, :, 2560:2624]),
                (lambda t: v8(t)[:, :, 64:128], w_in_v[:, :, 2560:2624]),
                (lambda t: v8(t)[:, :, 128:136], w_in_v[:, :, 2624:2632]),
            ]
        else:
            c0 = WB_COL[name]
            parts = [(lambda t: v8(t), w_in_v[:, :, c0:c0 + 512])]
        stage_weight(parts, wq_in[bi])
    v4h = lambda t: t[:, 0:2048].rearrange("p (k c) -> p k c", k=4)
    for hf in range(2):
        stage_weight([(v4h, w_a_d.rearrange("(k p) c -> p k c", p=128)[:, :, hf * 512:(hf + 1) * 512])], wq_a[hf])
        stage_weight([(v4h, w_b_d.rearrange("(k p) c -> p k c", p=128)[:, :, hf * 512:(hf + 1) * 512])], wq_b[hf])
    for hf in range(2):
        stage_weight([(v8, w_out_d.rearrange("(k p) c -> p k c", p=128)[:, :, hf * 512:(hf + 1) * 512])], wq_out[hf])
        stage_weight([(v8, w_gate_d.rearrange("(k p) c -> p k c", p=128)[:, :, hf * 512:(hf + 1) * 512])], wq_gate[hf])
    stage_weight([(v2, w_proj_d.rearrange("(k p) c -> p k c", p=128))], wq_proj)

    wl = [0]

    def load_w(src, n=4096):
        i = wl[0] % 3
        wl[0] += 1
        sc.dma("sp", lambda e, i=i, src=src, n=n: e.dma_start(out=wblk[i][:, 0:n], in_=src),
               reads=["wq"], writes=[f"wblk{i}"])
        return wblk[i], f"wblk{i}"

    def rms_rstd(col0):
        for s in range(SUB):
            sc.act(lambda e, s=s: e.activation(out=Mq[:, D:2 * D], in_=xs[:, s, :], func=AF.Square,
                                               accum_out=stat[:, 8 + s:9 + s]),
                   reads=bigkeys(s * D, (s + 1) * D), writes=["Mq", f"st{8 + s}"])
        sc.dve(lambda e: e.tensor_scalar(out=stat[:, 10:10 + SUB], in0=stat[:, 8:8 + SUB], scalar1=1.0 / D,
                                         scalar2=EPS, op0=ALU.mult, op1=ALU.add),
               reads=[f"st{8 + s}" for s in range(SUB)], writes=["st10"])
        sc.pool(lambda e: e.tensor_tensor(out=stat[:, col0:col0 + SUB], in0=stat[:, 10:10 + SUB], in1=neghalf[:, 0:SUB],
                                          op=ALU.pow), reads=["st10", "neghalf"], writes=[f"rstd{col0}"])
        return f"rstd{col0}"

    def norm_transpose(rkey, col0, gvec, gkey, dstT, dkey):
        for s in range(SUB):
            sc.dve(lambda e, s=s: e.tensor_scalar(out=xn, in0=xs[:, s, :], scalar1=stat[:, col0 + s:col0 + s + 1],
                                                  scalar2=None, op0=ALU.mult),
                   reads=bigkeys(s * D, (s + 1) * D) + [rkey], writes=["Mq"])
            pt_, pk = ps_full()
            ptb = pt_[:].bitcast(BF16)
            for kc in range(KC):
                sc.pe(lambda e, kc=kc, ptb=ptb: e.transpose(out=ptb[:, kc * 128:(kc + 1) * 128],
                                                           in_=xn[:, kc * 128:(kc + 1) * 128], identity=identb[:]),
                      reads=["Mq", "identb"], writes=pk)
            sc.dve(lambda e, s=s, ptb=ptb: e.tensor_tensor(
                out=dstT[:, :, s * 128:(s + 1) * 128], in0=ptb.rearrange("p (k t) -> p k t", k=KC),
                in1=gvec[:, :].unsqueeze(2).to_broadcast([128, KC, 128]), op=ALU.mult),
                reads=pk + [gkey], writes=[dkey])

    def conv_gen(ccs):
        for cc in ccs:
            sc.dve(lambda e, cc=cc: e.tensor_scalar(out=cacc[:, cc, :], in0=uT[:, cc, 0:G], scalar1=convw[:, cc, 0:1],
                                                    scalar2=convb[:, cc:cc + 1], op0=ALU.mult, op1=ALU.add),
                   reads=["uT", "uTh", "convw", "convb"], writes=[f"cacc{cc}"])
        for jj in range(1, CONVK):
            for cc in ccs:
                sc.dve(lambda e, cc=cc, jj=jj: e.scalar_tensor_tensor(
                    out=cacc[:, cc, :], in0=uT[:, cc, jj:jj + G], scalar=convw[:, cc, jj:jj + 1], in1=cacc[:, cc, :],
                    op0=ALU.mult, op1=ALU.add), reads=["uT", "uTh", "convw", f"cacc{cc}"], writes=[f"cacc{cc}"])
            yield

    def head(g):
        t0 = g * G
        hT = hT2[g % 2]
        hTk = f"hT{g % 2}"
        stg = Mq[:, 2048:4096].bitcast(F32)
        for s in range(SUB):
            sc.dma("pool", lambda e, t0=t0, s=s: e.dma_start(out=stg, in_=x_d[t0 + s * 128:t0 + (s + 1) * 128, :]),
                   writes=["MqH"])
            sc.act(lambda e: e.activation(out=Mq[:, D:2 * D], in_=stg, func=AF.Square, accum_out=stat[:, 16:17]),
                   reads=["MqH"], writes=["Mq", "st16"])
            sc.dve(lambda e: e.tensor_scalar(out=stat[:, 17:18], in0=stat[:, 16:17], scalar1=1.0 / D, scalar2=EPS,
                                             op0=ALU.mult, op1=ALU.add), reads=["st16"], writes=["st17"])
            sc.pool(lambda e: e.tensor_tensor(out=stat[:, 18:19], in0=stat[:, 17:18], in1=neghalf[:, 0:1], op=ALU.pow),
                    reads=["st17", "neghalf"], writes=["st18"])
            sc.dve(lambda e: e.tensor_scalar(out=xn, in0=stg, scalar1=stat[:, 18:19], scalar2=None, op0=ALU.mult),
                   reads=["MqH", "st18"], writes=["Mq"])
            pt_, pk = ps_full()
            ptb = pt_[:].bitcast(BF16)
            for kc in range(KC):
                sc.pe(lambda e, kc=kc, ptb=ptb: e.transpose(out=ptb[:, kc * 128:(kc + 1) * 128],
                                                           in_=xn[:, kc * 128:(kc + 1) * 128], identity=identb[:]),
                      reads=["Mq", "identb"], writes=pk)
            sc.dve(lambda e, s=s, ptb=ptb, hT=hT: e.tensor_tensor(
                out=hT[:, :, s * 128:(s + 1) * 128], in0=ptb.rearrange("p (k t) -> p k t", k=KC),
                in1=gin[:, :].unsqueeze(2).to_broadcast([128, KC, 128]), op=ALU.mult),
                reads=pk + ["gin"], writes=[hTk])
            yield
        def fm_block(wt, wk, ncc, epi, col_of=lambda cc: cc * 128):
            wv = v8(wt)
            chk('mm')
            for cc in range(ncc):
                pt_, pk = ps_half()
                c0 = col_of(cc)
                for kc in range(KC):
                    sc.pe(lambda e, kc=kc, c0=c0, pt_=pt_, wv=wv: e.matmul(
                        pt_, lhsT=wv[:, kc, c0:c0 + 128], rhs=hT[:, kc, :], start=(kc == 0), stop=(kc == KC - 1)),
                        reads=[wk, hTk], writes=pk)
                chk('epi')
                epi(cc, pt_, pk)

        def epi_copy(dst_fn, dkey):
            def f(cc, pt_, pk):
                copy_any(dst_fn(cc), pt_, pk, [dkey])
            return f

        def epi_act(dst_fn, dkey, func):
            def f(cc, pt_, pk):
                sc.act(lambda e: e.activation(out=dst_fn(cc), in_=pt_, func=func), pk, [dkey])
            return f

        cg = None
        for name in ("glua", "glub", "q", "k", "v", "za", "qi", "kiwi", "zb"):
            bi = WBLOCKS.index(name)
            wt, wk = load_w(wq_in[bi])
            if name == "q":
                def epi_q(cc, pt_, pk):
                    copy_any(qTe[0:64, cc, :], pt_[0:64, :], pk, ["qTe"])
                    copy_any(qTo[64:128, cc, :], pt_[64:128, :], pk, ["qTo"])
                fm_block(wt, wk, 4, epi_q)
            elif name == "k":
                fm_block(wt, wk, 4, epi_copy(lambda cc, t0=t0: kT[:, cc, t0:t0 + G], f"kT{g}"))
            elif name == "v":
                wv = v8(wt)
                for s in range(SUB):
                    pt_, pk = ps_full()
                    for kc in range(KC):
                        sc.pe(lambda e, kc=kc, s=s, pt_=pt_, wv=wv: e.matmul(
                            pt_[:], lhsT=hT[:, kc, s * 128:(s + 1) * 128], rhs=wv[:, kc, :],
                            start=(kc == 0), stop=(kc == KC - 1)), reads=[wk, hTk], writes=pk)
                    j = g * SUB + s
                    pv = pt_[:].rearrange("p (c e d) -> p c e d", c=4, e=2)
                    copy_any(vaug[:, j, :, 0:64], pv[:, :, 0, :], pk, [f"vaug{g}"])
                    copy_any(vaug[:, j, :, 96:160], pv[:, :, 1, :], pk, [f"vaug{g}"])
            elif name == "za":
                fm_block(wt, wk, 4, epi_act(lambda cc: szaT[:, cc, :], "szaT", AF.Silu))
            elif name == "qi":
                def epi_qi(cc, pt_, pk):
                    copy_any(qidxTe[0:64, cc, :], pt_[0:64, :], pk, ["qidxTe"])
                    copy_any(qidxTo[64:128, cc, :], pt_[64:128, :], pk, ["qidxTo"])
                fm_block(wt, wk, 4, epi_qi)
            elif name == "kiwi":
                fm_block(wt, wk, 1, epi_copy(lambda cc, t0=t0: kidx2[:, t0:t0 + G], f"kidx{g}"))
                wv = v8(wt)
                for s in range(SUB):
                    pt_, pk = ps_half()
                    for kc in range(KC):
                        sc.pe(lambda e, kc=kc, s=s, pt_=pt_, wv=wv: e.matmul(
                            pt_[:, 0:8], lhsT=hT[:, kc, s * 128:(s + 1) * 128], rhs=wv[:, kc, 128:136],
                            start=(kc == 0), stop=(kc == KC - 1)), reads=[wk, hTk], writes=pk)
                    sc.act(lambda e, s=s, pt_=pt_: e.activation(out=wabs[:, s, :], in_=pt_[:, 0:8], func=AF.Abs,
                                                                scale=IDX_SCALE), reads=pk, writes=["wabs"])
                    sc.act(lambda e, s=s, pt_=pt_: e.activation(out=wsgn[:, s, :], in_=pt_[:, 0:8], func=AF.Sign),
                           reads=pk, writes=["wsgn"])
            elif name == "glua":
                fm_block(wt, wk, 4, epi_copy(lambda cc: uT[:, cc, 30:30 + G], "uT"))
            elif name == "glub":
                def epi_glu(cc, pt_, pk):
                    sc.act(lambda e: e.activation(out=lnt[0][:], in_=pt_, func=AF.Sigmoid), pk, ["lnt0"])
                    sc.dve(lambda e: e.tensor_tensor(out=uT[:, cc, 30:30 + G], in0=uT[:, cc, 30:30 + G], in1=lnt[0][:],
                                                     op=ALU.mult), ["uT", "lnt0"], ["uT"])
                fm_block(wt, wk, 4, epi_glu)
            elif name == "zb":
                fm_block(wt, wk, 4, epi_act(lambda cc: szbT[:, cc, :], "szbT", AF.Silu))
            if name == "glub":
                cg = conv_gen([0, 1, 2, 3])
            elif cg is not None:
                for _ in range(6):
                    next(cg, None)
            yield
        for _ in cg:
            pass
        yield

    def middle(g):
        t0 = g * G
        for s in range(SUB):
            i = g * SUB + s
            n = 128 * (i + 1)
            nkb = (n + 511) // 512
            for kb in range(nkb):
                cols = min(512, n - kb * 512)
                kkeys = sorted({f"kidx{(kb * 512 + o) // G}" for o in range(0, cols, 128)})
                for h in range(8):
                    c, eo = h // 2, h % 2
                    pt_, pk = ps_full()
                    qsel = qidxTe if eo == 0 else qidxTo
                    sc.pe(lambda e, c=c, qsel=qsel, s=s, kb=kb, cols=cols, pt_=pt_: e.matmul(
                        pt_[:, 0:cols], lhsT=qsel[:, c, s * 128:(s + 1) * 128],
                        rhs=kidx2[:, kb * 512:kb * 512 + cols], start=True, stop=True),
                        reads=["qidxTe", "qidxTo"] + kkeys, writes=pk)
                    rb = Rb[h % 2]
                    rk_ = f"lnt{2 * (h % 2)}"
                    rk2_ = f"lnt{2 * (h % 2) + 1}"
                    sc.act(lambda e, s=s, h=h, cols=cols, pt_=pt_, rb=rb: e.activation(
                        out=rb[:, 0:cols], in_=pt_[:, 0:cols], func=AF.Relu, scale=wabs[:, s, h:h + 1]),
                        reads=pk + ["wabs"], writes=[rk_, rk2_])
                    ak = bigkeys(kb * 512, kb * 512 + cols)
                    if h == 0:
                        sc.dve(lambda e, s=s, kb=kb, cols=cols, rb=rb: e.tensor_scalar(
                            out=acc[:, kb * 512:kb * 512 + cols], in0=rb[:, 0:cols], scalar1=wsgn[:, s, 0:1],
                            scalar2=None, op0=ALU.mult), reads=[rk_, rk2_, "wsgn"], writes=ak)
                    else:
                        sc.dve(lambda e, s=s, h=h, kb=kb, cols=cols, rb=rb: e.scalar_tensor_tensor(
                            out=acc[:, kb * 512:kb * 512 + cols], in0=rb[:, 0:cols], scalar=wsgn[:, s, h:h + 1],
                            in1=acc[:, kb * 512:kb * 512 + cols], op0=ALU.mult, op1=ALU.add),
                            reads=[rk_, rk2_, "wsgn"] + ak, writes=ak)
            akall = bigkeys(0, n)
            sc.dve(lambda e, n=n: e.tensor_reduce(out=prb[:, 3:4], in_=acc[:, 0:n], axis=AX.X, op=ALU.max,
                                                  apply_absolute_value=True), reads=akall, writes=["prb3"])
            sc.dve(lambda e: e.tensor_scalar(out=prb[:, 3:4], in0=prb[:, 3:4], scalar1=2.0, scalar2=2.0,
                                             op0=ALU.mult, op1=ALU.add), reads=["prb3"], writes=["prb3"])
            sc.dve(lambda e: e.tensor_scalar(out=bis[:, 0:NIT + 2], in0=pow2[:], scalar1=prb[:, 3:4], scalar2=None,
                                             op0=ALU.mult), reads=["prb3", "pow2"], writes=["bis"])
            sc.dve(lambda e: e.tensor_scalar(out=nbis[:, 0:NIT + 2], in0=pow2[:], scalar1=prb[:, 3:4], scalar2=-1.0,
                                             op0=ALU.mult, op1=ALU.mult), reads=["prb3", "pow2"], writes=["nbis"])
            sc.dve(lambda e: e.memset(prb[:, 0:1], 0.0), writes=["prb0"])
            sc.dve(lambda e, n=n: e.memset(prb[:, 4:5], float(n - 2 * TOPK + 1)), writes=["prb4"])
            sc.dve(lambda e, n=n: e.memset(acc[0:64, n - 64:n], NEG), reads=["prb3"], writes=bigkeys(n - 64, n))
            split = n >= 1024
            nA = (n // 2) // 128 * 128 if split else n
            nB = n - nA
            junkB = lnt_all[:].rearrange("p a b -> p (a b)").bitcast(BF16)
            lkeys = ["lnt0", "lnt1", "lnt2", "lnt3"]
            for it in range(1, NIT + 1):
                pi, po = (it - 1) % 2, it % 2
                sc.act(lambda e, nA=nA, pi=pi: e.activation(
                    out=Mq[:, 0:nA], in_=acc[:, 0:nA], func=AF.Sign, bias=prb[:, pi:pi + 1], scale=1.0,
                    accum_out=prb[:, 2:3]), reads=akall + [f"prb{pi}"], writes=["Mq", "MqH", "prb2"])
                if split:
                    sc.dve(lambda e, pi=pi: e.tensor_scalar(out=prb[:, 8:9], in0=prb[:, pi:pi + 1], scalar1=-1.0,
                                                            scalar2=None, op0=ALU.mult),
                           reads=[f"prb{pi}"], writes=["prb8"])
                    sc.dve(lambda e, nA=nA, n=n, nB=nB: e.tensor_scalar(
                        out=junkB[:, 0:nB], in0=acc[:, nA:n], scalar1=prb[:, 8:9], scalar2=None,
                        op0=ALU.is_ge, op1=ALU.add, accum_out=prb[:, 6:7]),
                        reads=akall + ["prb8"], writes=lkeys + ["prb6"])
                    sc.dve(lambda e, nA=nA: e.tensor_scalar(out=prb[:, 7:8], in0=prb[:, 6:7], scalar1=2.0,
                                                            scalar2=float(nA - 2 * TOPK + 1), op0=ALU.mult, op1=ALU.add),
                           reads=["prb6"], writes=["prb7"])
                    bcol, bkey = 7, "prb7"
                else:
                    bcol, bkey = 4, "prb4"
                sc.act(lambda e, bcol=bcol: e.activation(out=prb[:, 3:4], in_=prb[:, 2:3], func=AF.Sign,
                                                         bias=prb[:, bcol:bcol + 1], scale=1.0),
                       reads=["prb2", bkey], writes=["prb3"])
                sc.act(lambda e, it=it, pi=pi, po=po: e.activation(
                    out=prb[:, po:po + 1], in_=prb[:, 3:4], func=AF.Identity, scale=nbis[:, it + 1:it + 2],
                    bias=prb[:, pi:pi + 1]), reads=["prb3", "nbis", f"prb{pi}"], writes=[f"prb{po}"])
            pf = NIT % 2
            sc.dve(lambda e, pf=pf: e.tensor_scalar(out=prb[:, 5:6], in0=prb[:, pf:pf + 1], scalar1=-1.0,
                                                    scalar2=bis[:, NIT + 1:NIT + 2], op0=ALU.mult, op1=ALU.subtract),
                   reads=[f"prb{pf}", "bis"], writes=["prb5"])
            sc.dve(lambda e, n=n: e.tensor_scalar(out=Mq[:, 0:n], in0=acc[:, 0:n], scalar1=prb[:, 5:6],
                                                  scalar2=None, op0=ALU.is_ge),
                   reads=akall + ["prb5"], writes=["Mq", "MqH"])
            for jb0 in range(0, i + 1, 8):
                nb = min(8, i + 1 - jb0)
                pt_, pk = ps_full()
                ptb = pt_[:].bitcast(BF16)
                for k in range(nb):
                    sc.pe(lambda e, k=k, jb0=jb0, ptb=ptb: e.transpose(
                        out=ptb[:, k * 128:(k + 1) * 128], in_=Mq[:, (jb0 + k) * 128:(jb0 + k + 1) * 128],
                        identity=identb[:]), reads=["Mq", "MqH", "identb"], writes=pk)
                copy_any(MT[:, jb0:jb0 + nb, s * 128:(s + 1) * 128],
                         ptb[:, 0:nb * 128].rearrange("p (k t) -> p k t", k=nb), pk, ["MT"])

        dbg('acc', lambda: acc[:, 0:256], bigkeys(0, 256), 256)
        dbg('Mq', lambda: Mq[:, 0:256], ['Mq'], 256)
        dbg('MT', lambda: MT[:, 0:2, :].rearrange('p j t -> p (j t)'), ['MT'], 2 * G)
        dbg('prb', lambda: prb[:, 0:4], ['prb0', 'prb1', 'prb2', 'prb3'], 4)
        chk(f'e{g}')
        jlast = g * SUB + SUB - 1
        ptr = [0]
        LA = 5
        HPP = 2
        for hp in range(8 // HPP):
            items = [(j, hh) for j in range(jlast + 1) for hh in range(HPP)]
            ltile = {}

            def emit_qk(it, hp=hp):
                j, hh = it
                m = j - g * SUB
                q0 = max(m, 0) * 128
                Tq = G - q0
                jg = j // SUB
                h = HPP * hp + hh
                c, eo = h // 2, h % 2
                lt, lk = ps_att()
                qsel = qTe if eo == 0 else qTo
                sc.pe(lambda e, c=c, qsel=qsel, j=j, q0=q0, Tq=Tq, lt=lt: e.matmul(
                    lt[:, 0:Tq], lhsT=kT[:, c, j * 128:(j + 1) * 128],
                    rhs=qsel[:, c, q0:G], start=True, stop=True),
                    reads=[f"kT{jg}", "qTe", "qTo"], writes=lk)
                ltile[it] = (lt, lk)

            def emit_rest(it, hp=hp, jlast=jlast):
                j, hh = it
                lt, lk = ltile.pop(it)
                m = j - g * SUB
                q0 = max(m, 0) * 128
                Tq = G - q0
                jg = j // SUB
                h = HPP * hp + hh
                c, eo = h // 2, h % 2
                pti = ptr[0] % 5
                ptr[0] += 1
                P = PT[pti]
                pkey = f"PT{pti}"
                ib0 = max(m, 0)
                far0 = max(m + 2, 0)
                for ib in range(ib0, min(far0, SUB)):
                    dsel = d0t if ib == m else d1t
                    dk = "d0t" if ib == m else "d1t"
                    lo = (ib - ib0) * 128
                    td = tmpD[ib % 2]
                    tk = f"tmpD{ib % 2}"
                    sc.dve(lambda e, lo=lo, h=h, lt=lt, td=td, dsel=dsel: e.scalar_tensor_tensor(
                        out=td[:], in0=lt[:, lo:lo + 128], scalar=0.125, in1=dsel[:, h, :],
                        op0=ALU.mult, op1=ALU.add), reads=lk + [dk], writes=[tk])
                    sc.act(lambda e, lo=lo, P=P, td=td: e.activation(out=P[:, lo:lo + 128], in_=td[:], func=AF.Exp),
                           reads=[tk], writes=[pkey])
                if far0 < SUB:
                    lo = (far0 - ib0) * 128
                    sc.act(lambda e, lo=lo, Tq=Tq, h=h, P=P, lt=lt: e.activation(
                        out=P[:, lo:Tq], in_=lt[:, lo:Tq], func=AF.Exp, bias=b15[:, h:h + 1], scale=0.125),
                        reads=lk + ["b15"], writes=[pkey])
                mul_eng = sc.pool if (ptr[0] % 4 == 0) else sc.dve
                mul_eng(lambda e, j=j, q0=q0, Tq=Tq, P=P: e.tensor_tensor(
                    out=P[:, 0:Tq], in0=P[:, 0:Tq], in1=MT[:, j, q0:G], op=ALU.mult),
                    reads=[pkey, "MT"], writes=[pkey])
                ab = ps[hh]
                akey = [f"ps{hh}a", f"ps{hh}b"]
                if eo == 0:
                    sc.pe(lambda e, j=j, c=c, q0=q0, Tq=Tq, P=P, ab=ab, jlast=jlast: e.matmul(
                        ab[0:65, q0:G], lhsT=vaug[:, j, c, 0:65], rhs=P[:, 0:Tq],
                        start=(j == 0), stop=(j == jlast)),
                        reads=[pkey, f"vaug{jg}", "vaug_c"], writes=akey)
                else:
                    sc.pe(lambda e, j=j, c=c, q0=q0, Tq=Tq, P=P, ab=ab, jlast=jlast: e.matmul(
                        ab[:, q0:G], lhsT=vaug[:, j, c, 32:160], rhs=P[:, 0:Tq],
                        start=(j == 0), stop=(j == jlast)),
                        reads=[pkey, f"vaug{jg}", "vaug_c"], writes=akey)

            for it in items[:LA]:
                emit_qk(it)
            for ii, it in enumerate(items):
                emit_rest(it)
                if ii + LA < len(items):
                    emit_qk(items[ii + LA])
            for hh in range(HPP):
                h = HPP * hp + hh
                c, eo = h // 2, h % 2
                ab = ps[hh]
                akey = [f"ps{hh}a", f"ps{hh}b"]
                dr = 64 if eo == 0 else 32
                r0 = 0 if eo == 0 else 64
                hk = f"{eo}"
                sc.dve(lambda e, dr=dr, ab=ab: e.reciprocal(out=rden[dr:dr + 1, :], in_=ab[dr:dr + 1, 0:G]),
                       reads=akey, writes=[f"rden{hk}"])
                bt, bk = ps_half()
                selm = sel_e if eo == 0 else sel_o
                sc.pe(lambda e, bt=bt, selm=selm: e.matmul(bt, lhsT=selm[:], rhs=rden[:], start=True, stop=True),
                      reads=["rden0", "rden1", "sel"], writes=bk)
                sc.dve(lambda e, r0=r0, bt=bt: e.tensor_copy(out=bcs[r0:r0 + 64, :], in_=bt[r0:r0 + 64, :]),
                       reads=bk, writes=["lnt1"])
                sc.dve(lambda e, r0=r0, ab=ab: e.tensor_tensor(out=t1[r0:r0 + 64, :], in0=ab[r0:r0 + 64, 0:G],
                                                               in1=bcs[r0:r0 + 64, :], op=ALU.mult),
                       reads=akey + ["lnt1"], writes=["lnt2"])
                sc.dve(lambda e, r0=r0, c=c: e.tensor_tensor(out=gatedT[r0:r0 + 64, c, :], in0=t1[r0:r0 + 64, :],
                                                             in1=szaT[r0:r0 + 64, c, :], op=ALU.mult),
                       reads=["lnt2", "szaT"], writes=["gatedT"])

        dbg('gatedT', lambda: gatedT[:].rearrange('p k t -> p (k t)'), ['gatedT'], 1024)
        dbg('ps0pre', lambda: ps[0][:, 0:256], ['ps0a', 'ps0b'], 256)
        chk(f'f{g}')
    def tail(g):
        t0 = g * G
        hT = hT2[g % 2]
        hTk = f"hT{g % 2}"
        sc.dma("pool", lambda e, t0=t0: e.dma_start(out=ptl[:], in_=p_d[t0:t0 + G, :].rearrange("(s p) d -> p s d", p=128)),
               writes=["ptl"])
        if os.environ.get("MK_DBG") == "cacc":
            sc.dma("sp", lambda e: e.dma_start(out=out_d[0:128, :], in_=cacc[:].rearrange("p c t -> p (c t)")),
                   reads=[f"cacc{c_}" for c_ in range(4)])
        if os.environ.get("MK_DBG") == "uT":
            sc.dma("sp", lambda e: e.dma_start(out=out_d[0:128, :], in_=uT[:, :, 0:256].rearrange("p c t -> p (c t)")),
                   reads=["uT", "uTh"])
        chk('fconv')
        sc.pool(lambda e: e.tensor_copy(out=uT[:, :, 0:30], in_=uT[:, :, G:G + 30]), reads=["uT"], writes=["uTh"])
        chk('fhalo')
        def stat_mm(dst_t, dst_k, src_fn, skeys_fn):
            for cc in range(4):
                hi, lo = hl[0][cc % 2], hl[1][cc % 2]
                hk_, lk_ = f"PT{cc % 2}", f"PT{2 + cc % 2}"
                sc.dve(lambda e, cc=cc, hi=hi: e.tensor_copy(out=hi[:], in_=src_fn(cc)),
                       reads=skeys_fn(cc), writes=[hk_])
                sc.dve(lambda e, cc=cc, hi=hi, lo=lo: e.tensor_tensor(out=lo[:], in0=src_fn(cc), in1=hi[:], op=ALU.subtract),
                       reads=skeys_fn(cc) + [hk_], writes=[lk_])
                sc.pe(lambda e, cc=cc, hi=hi: e.matmul(dst_t, lhsT=oneslnb[:], rhs=hi[:], start=(cc == 0), stop=False),
                      reads=[hk_, "oneslnb"], writes=dst_k)
                if cc == 0:
                    dbg('mm1', lambda: dst_t, dst_k, G)
                sc.pe(lambda e, cc=cc, lo=lo: e.matmul(dst_t, lhsT=oneslnb[:], rhs=lo[:], start=False, stop=(cc == 3)),
                      reads=[lk_, "oneslnb"], writes=dst_k)
                if cc == 0:
                    dbg('mm2', lambda: dst_t, dst_k, G)
                    dbg('lo0', lambda: lo[:], [lk_], G)
                if dbg_name == "mmall" and sc.enabled and dst_t is mean_t:
                    for half_, src_ in ((0, hi), (1, lo)):
                        pass
                    k_ = cc
                    sc.dve(lambda e: e.tensor_copy(out=dbgt[:, 0:G], in_=dst_t), reads=dst_k, writes=["dbgt"])
                    sc.dve(lambda e, hi=hi: e.tensor_copy(out=dbgt[:, G:2 * G], in_=hi[:]), reads=[hk_], writes=["dbgt"])
                    sc.dve(lambda e, lo=lo: e.tensor_copy(out=dbgt[:, 2 * G:3 * G], in_=lo[:]), reads=[lk_], writes=["dbgt"])
                    sc.dma("sp", lambda e, k_=k_: e.dma_start(out=out_d[k_ * 128:(k_ + 1) * 128, 0:3 * G], in_=dbgt[:, 0:3 * G]),
                           reads=["dbgt"])
                    if cc == 3:
                        sc.enabled = False

        mean_t, mean_k = ps_half(os.environ.get('MK_MB', 'a'))
        stat_mm(mean_t, mean_k, lambda cc: cacc[:, cc, :], lambda cc: [f"cacc{cc}"])
        dbg('onesb', lambda: oneslnb[:], ['oneslnb'], 128)
        dbg('meanps', lambda: mean_t, mean_k, G)
        chk('fmean')
        ex2_t, ex2_k = ps_half(os.environ.get('MK_MB', 'a'))
        for cc in range(4):
            sc.act(lambda e, cc=cc: e.activation(out=sqb[:, cc, :], in_=cacc[:, cc, :], func=AF.Square),
                   reads=[f"cacc{cc}"], writes=[f"mg{2 * cc}", f"mg{2 * cc + 1}"])
        stat_mm(ex2_t, ex2_k, lambda cc: sqb[:, cc, :], lambda cc: [f"mg{2 * cc}", f"mg{2 * cc + 1}"])
        chk('fex2')
        if os.environ.get("MK_X") == "1":
            sc.dve(lambda e: e.tensor_copy(out=lnt[2][:], in_=lnt[4][:]), reads=mean_k, writes=["lnt2"])
        elif os.environ.get("MK_X") == "2":
            sc.dve(lambda e: e.tensor_copy(out=mab[0][:], in_=mean_t), reads=mean_k, writes=["lnt2"])
        else:
            sc.dve(lambda e, mean_t=mean_t: e.tensor_copy(out=lnt[2][:], in_=mean_t), reads=mean_k, writes=["lnt2"])
        sc.dve(lambda e: e.tensor_tensor(out=lnt[3][:], in0=lnt[2][:], in1=lnt[2][:], op=ALU.mult),
               reads=["lnt2"], writes=["lnt3"])
        sc.dve(lambda e, ex2_t=ex2_t: e.tensor_tensor(out=lnt[3][:], in0=ex2_t, in1=lnt[3][:], op=ALU.subtract),
               reads=ex2_k + ["lnt3"], writes=["lnt3"])
        sc.dve(lambda e: e.tensor_scalar(out=lnt[3][:], in0=lnt[3][:], scalar1=EPS, scalar2=None, op0=ALU.add),
               reads=["lnt3"], writes=["lnt3"])
        sc.act(lambda e: e.activation(out=lnt[3][:], in_=lnt[3][:], func=AF.Sqrt), reads=["lnt3"], writes=["lnt3"])
        sc.dve(lambda e: e.reciprocal(out=lnt[4][:], in_=lnt[3][:]), reads=["lnt3"], writes=["lnt4"])
        chk('frstd')
        for cc in range(4):
            sc.dve(lambda e, cc=cc: e.tensor_tensor(out=lnt[cc % 2][:], in0=cacc[:, cc, :], in1=lnt[2][:], op=ALU.subtract),
                   reads=[f"cacc{cc}", "lnt2"], writes=[f"lnt{cc % 2}"])
            sc.dve(lambda e, cc=cc: e.tensor_tensor(out=lnt[cc % 2][:], in0=lnt[cc % 2][:], in1=lnt[4][:], op=ALU.mult),
                   reads=[f"lnt{cc % 2}", "lnt4"], writes=[f"lnt{cc % 2}"])
            sc.act(lambda e, cc=cc: e.activation(out=mab[cc % 2][:], in_=lnt[cc % 2][:], func=AF.Silu,
                                                 bias=lnb[:, cc:cc + 1], scale=lng[:, cc:cc + 1]),
                   reads=[f"lnt{cc % 2}", "lng", "lnb"], writes=[f"mab{cc % 2}"])
            sc.dve(lambda e, cc=cc: e.tensor_tensor(out=gatedbT[:, cc, :], in0=mab[cc % 2][:], in1=szbT[:, cc, :], op=ALU.mult),
                   reads=[f"mab{cc % 2}", "szbT"], writes=["gatedbT"])

        dbg('cacc', lambda: cacc[:].rearrange('p k t -> p (k t)'), [f'cacc{c_}' for c_ in range(4)], 1024)
        dbg('gatedbT', lambda: gatedbT[:].rearrange('p k t -> p (k t)'), ['gatedbT'], 1024)
        dbg('lnt2', lambda: lnt[2][:], ['lnt2'], G)
        dbg('lnt4', lambda: lnt[4][:], ['lnt4'], G)
        yield
        for (gname, srcT, skey, wq_x) in (("gta", gatedT, "gatedT", wq_a), ("gtb", gatedbT, "gatedbT", wq_b)):
            wvb = wbr[:].rearrange("p (k c) -> p k c", k=4)
            wkb = "wbr"
            for hf in range(2):
                sc.dma("sp", lambda e, hf=hf, wq_x=wq_x: e.dma_start(out=wbr[:], in_=wq_x[hf]),
                       reads=["wq"], writes=["wbr"])
                bi = WBLOCKS.index(f"{gname}{hf}")
                wtg, wkg = load_w(wq_in[bi])
                wvg = v8(wtg)
                for cc in range(4):
                    dm = hf * 4 + cc
                    gt_, gk_ = ps_half()
                    for kc in range(KC):
                        sc.pe(lambda e, kc=kc, cc=cc, gt_=gt_, wvg=wvg: e.matmul(
                            gt_, lhsT=wvg[:, kc, cc * 128:(cc + 1) * 128], rhs=hT[:, kc, :],
                            start=(kc == 0), stop=(kc == KC - 1)), reads=[wkg, hTk], writes=gk_)
                    gb_ = gtmp[dm % 2]
                    gbk = f"gtmp{dm % 2}"
                    sc.act(lambda e, gt_=gt_, gb_=gb_: e.activation(out=gb_[:], in_=gt_, func=AF.Sigmoid),
                           reads=gk_, writes=[gbk])
                    yt_, yk_ = ps_half()
                    for kc in range(4):
                        sc.pe(lambda e, kc=kc, cc=cc, yt_=yt_, wvb=wvb, srcT=srcT: e.matmul(
                            yt_, lhsT=wvb[:, kc, cc * 128:(cc + 1) * 128], rhs=srcT[:, kc, :],
                            start=(kc == 0), stop=(kc == 3)), reads=[wkb, skey], writes=yk_)
                    if gname == "gta":
                        sc.dve(lambda e, dm=dm, yt_=yt_, gb_=gb_: e.tensor_tensor(
                            out=mergedT[:, dm, :], in0=yt_, in1=gb_[:], op=ALU.mult),
                            reads=yk_ + [gbk], writes=[f"mg{dm}"])
                    else:
                        mb_ = mab[dm % 2]
                        mk_ = f"mab{dm % 2}"
                        sc.dve(lambda e, yt_=yt_, gb_=gb_, mb_=mb_: e.tensor_tensor(
                            out=mb_[:], in0=yt_, in1=gb_[:], op=ALU.mult), reads=yk_ + [gbk], writes=[mk_])
                        sc.dve(lambda e, dm=dm, mb_=mb_: e.tensor_tensor(
                            out=mergedT[:, dm, :], in0=mergedT[:, dm, :], in1=mb_[:], op=ALU.add),
                            reads=[mk_, f"mg{dm}"], writes=[f"mg{dm}"])
                    yield
        sc.dma("pool", lambda e, t0=t0: e.dma_start(out=xs, in_=x_d[t0:t0 + G, :].rearrange("(s p) d -> p s d", p=128)),
               writes=bigkeys(0, SUB * D))
        mgk = [f"mg{dm}" for dm in range(KC)]
        for hf in range(2):
            wt, wk = load_w(wq_out[hf])
            wv = v8(wt)
            for s in range(SUB):
                pt_, pk = ps_full("a")
                for kc in range(KC):
                    sc.pe(lambda e, kc=kc, s=s, pt_=pt_, wv=wv: e.matmul(
                        pt_[:], lhsT=mergedT[:, kc, s * 128:(s + 1) * 128], rhs=wv[:, kc, :],
                        start=(kc == 0), stop=(kc == KC - 1)), reads=[wk] + mgk, writes=pk)
                xk = bigkeys(s * D + hf * 512, s * D + hf * 512 + 512)
                sc.dve(lambda e, s=s, hf=hf, pt_=pt_: e.tensor_tensor(
                    out=xs[:, s, hf * 512:(hf + 1) * 512], in0=xs[:, s, hf * 512:(hf + 1) * 512], in1=pt_[:], op=ALU.add),
                    reads=pk + xk, writes=xk)

        dbg('mergedT', lambda: mergedT[:, 0:4, :].rearrange('p k t -> p (k t)'), [f'mg{d_}' for d_ in range(8)], 1024)
        yield "B"
        sc.act(lambda e: e.activation(out=pbf[:], in_=ptl[:], func=AF.Identity), reads=["ptl"], writes=["pbf"])
        for s in range(SUB):
            pt_, pk = ps_half()
            ptb = pt_.bitcast(BF16)
            for k2 in range(2):
                sc.pe(lambda e, k2=k2, s=s, ptb=ptb: e.transpose(out=ptb[:, k2 * 128:(k2 + 1) * 128],
                                                               in_=pbf[:, s, k2 * 128:(k2 + 1) * 128], identity=identb[:]),
                      reads=["pbf", "identb"], writes=pk)
            copy_any(pT[:, :, s * 128:(s + 1) * 128], ptb[:, 0:256].rearrange("p (k t) -> p k t", k=2), pk, ["pT"])
        yield
        rk = rms_rstd(2)
        yield
        norm_transpose(rk, 2, gple, "gple", hT, hTk)
        yield
        sc.dma("sp", lambda e: e.dma_start(out=wbr[:], in_=wq_proj), reads=["wq"], writes=["wbr"])
        wkp = "wbr"
        wvp = wbr[:].rearrange("p (k c) -> p k c", k=2)
        for hf in range(2):
            wt, wk = load_w(wq_gate[hf])
            wv = v8(wt)
            for s in range(SUB):
                pt_, pk = ps_full("a")
                for kc in range(KC):
                    sc.pe(lambda e, kc=kc, s=s, pt_=pt_, wv=wv: e.matmul(
                        pt_[:], lhsT=hT[:, kc, s * 128:(s + 1) * 128], rhs=wv[:, kc, :],
                        start=(kc == 0), stop=(kc == KC - 1)), reads=[wk, hTk], writes=pk)
                sc.act(lambda e, pt_=pt_: e.activation(out=gsb, in_=pt_[:], func=AF.Sigmoid), reads=pk, writes=["big4"])
                et_, ek_ = ps_full("a")
                for k2 in range(2):
                    sc.pe(lambda e, k2=k2, s=s, hf=hf, et_=et_, wvp=wvp: e.matmul(
                        et_[:], lhsT=pT[:, k2, s * 128:(s + 1) * 128], rhs=wvp[:, k2, hf * 512:(hf + 1) * 512],
                        start=(k2 == 0), stop=(k2 == 1)), reads=[wkp, "pT"], writes=ek_)
                sc.dve(lambda e, et_=et_: e.tensor_tensor(out=tmpe, in0=et_[:], in1=gsb, op=ALU.mult),
                       reads=ek_ + ["big4"], writes=["big5"])
                xk = bigkeys(s * D + hf * 512, s * D + hf * 512 + 512)
                sc.dve(lambda e, s=s, hf=hf: e.tensor_tensor(
                    out=xs[:, s, hf * 512:(hf + 1) * 512], in0=xs[:, s, hf * 512:(hf + 1) * 512], in1=tmpe, op=ALU.add),
                    reads=["big5"] + xk, writes=xk)
                yield

        rk = rms_rstd(4)
        yield
        for s in range(SUB):
            sc.dve(lambda e, s=s: e.scalar_tensor_tensor(out=ob, in0=xs[:, s, :], scalar=stat[:, 4 + s:5 + s], in1=gfin[:],
                                                        op0=ALU.mult, op1=ALU.mult),
                   reads=bigkeys(s * D, (s + 1) * D) + [rk, "gfin"], writes=["big6", "big7"])
            sc.dma("sp", lambda e, s=s, t0=t0: e.dma_start(out=out_d[t0 + s * 128:t0 + (s + 1) * 128, :], in_=ob),
                   reads=["big6", "big7"])
            yield

        yield

    def _drain(gen, n=None):
        k = 0
        for _ in gen:
            k += 1
            if n is not None and k >= n:
                return False
        return True

    _drain(head(0))
    for g in range(NG):
        middle(g)
        tg = tail(g)
        hg = head(g + 1) if g + 1 < NG else iter(())
        nA = 0
        hx = 0
        for v in tg:
            if v == "B":
                break
            nA += 1
            if nA in (5, 9) and hx < SUB:
                next(hg, None)
                hx += 1
        while hx < SUB:
            next(hg, None)
            hx += 1
        td = hd = False
        _end = object()
        while not (td and hd):
            if not hd:
                hd = next(hg, _end) is _end
            if not td:
                td = next(tg, _end) is _end

    info = sc.emit(sems, dsems)
    st.close()
    return nc, info


def _t5_bucket(rel):
    half = 16
    max_exact = 8
    base = np.where(rel > 0, half, 0).astype(np.int32)
    n = np.abs(rel)
    nf = np.maximum(n, 1).astype(np.float32)
    large = max_exact + (np.log(nf / np.float32(max_exact)) / np.float32(math.log(128 / 8))
                         * np.float32(half - max_exact)).astype(np.int32)
    large = np.minimum(large, half - 1)
    return base + np.where(n < max_exact, n, large)


def host_consts(inputs):
    f = lambda a: np.ascontiguousarray(np.asarray(a, dtype=np.float32))
    rel_bias = f(inputs["rel_bias"])
    s_idx = np.arange(128)[:, None]
    t_idx = np.arange(128)[None, :]
    b0 = _t5_bucket((s_idx - t_idx).astype(np.int32))
    b1 = _t5_bucket((s_idx - 128 - t_idx).astype(np.int32))
    d0t = rel_bias[b0]
    d1t = rel_bias[b1]
    c = dict(
        w_in=f(inputs["w_in"][0]), w_a=f(inputs["w_branch_a"][0]), w_b=f(inputs["w_branch_b"][0]),
        w_out=f(inputs["w_out"][0]), w_gate=f(inputs["w_ple_gate"][0]), w_proj=f(inputs["w_ple_proj"][0]),
        gin=f(np.asarray(inputs["norm_in_g"][0]).reshape(KC, 128).T),
        gple=f(np.asarray(inputs["ple_norm_g"][0]).reshape(KC, 128).T),
        gfin=f(np.broadcast_to(np.asarray(inputs["final_norm_g"])[None, :], (128, D))),
        convw=f(np.asarray(inputs["conv_w"][0, :, 0, :]).T.reshape(4, 128, CONVK).transpose(1, 0, 2)),
        convb=f(np.asarray(inputs["conv_b"][0]).reshape(4, 128).T),
        lng=f(np.asarray(inputs["conv_ln_g"][0]).reshape(4, 128).T),
        lnb=f(np.asarray(inputs["conv_ln_b"][0]).reshape(4, 128).T),
        d0t=f(d0t.transpose(0, 2, 1)), d1t=f(d1t.transpose(0, 2, 1)),
        b15=f(np.broadcast_to(rel_bias[15][None, :], (128, 8))),
        ident=np.eye(128, dtype=np.float32),
        pow2=f(np.broadcast_to((2.0 ** -np.arange(NIT + 2, dtype=np.float64))[None, :], (128, NIT + 2))),
    )
    return c


_CACHE = {}


def kernel(**inputs):
    x = np.asarray(inputs["x"], dtype=np.float32)
    p = np.asarray(inputs["p"], dtype=np.float32)
    B, S, _ = x.shape
    if S not in _CACHE:
        _CACHE[S] = build(S)[0]
    nc = _CACHE[S]
    c = host_consts(inputs)
    in_maps = []
    for b in range(B):
        m = dict(c)
        m["x"] = np.ascontiguousarray(x[b])
        m["p"] = np.ascontiguousarray(p[0, b])
        in_maps.append(m)
    res = run_bass_kernel_spmd(nc, in_maps, core_ids=list(range(B)))
    return np.stack([np.asarray(r["out"], dtype=np.float32) for r in res.results], axis=0)
```

```python
import math
import os
from contextlib import ExitStack

import numpy as np
import concourse.bass as bass
import concourse.mybir as mybir
from concourse.bass_utils import run_bass_kernel_spmd

F32 = mybir.dt.float32
BF16 = mybir.dt.bfloat16
U8 = mybir.dt.uint8
AF = mybir.ActivationFunctionType
ALU = mybir.AluOpType
AX = mybir.AxisListType

D = 1024
KC = 8
G = 256
SUB = G // 128
NIT = 16
TOPK = 256
PLE = 256
CONVK = 31
IDX_SCALE = (64 ** -0.5) * (8 ** -0.5)
EPS = 1e-6
NEG = -1.0e30

SEG = dict(q=0, k=512, v=1024, za=1536, qi=2048, ki=2560, wi=2624, ga_=2632, gb_=3144,
           zb=3656, gta=4168, gtb=5192)
WBLOCKS = ["q", "k", "v", "za", "qi", "kiwi", "glua", "glub", "zb", "gta0", "gta1", "gtb0", "gtb1"]
WB_COL = dict(q=0, k=512, v=1024, za=1536, qi=2048, glua=2632, glub=3144, zb=3656,
              gta0=4168, gta1=4680, gtb0=5192, gtb1=5704)

COMPUTE = ("pe", "act", "dve", "pool")


class _Op:
    __slots__ = ("eng", "fn", "reads", "writes", "dma", "deps", "signal", "sig_idx",
                 "dsem", "dval", "idx")

    def __init__(self, eng, fn, reads, writes, dma):
        self.eng = eng
        self.fn = fn
        self.reads = tuple(reads)
        self.writes = tuple(writes)
        self.dma = dma
        self.deps = ()
        self.signal = False
        self.sig_idx = 0
        self.dsem = None
        self.dval = 0


class Sched:
    def __init__(self, nc):
        self.nc = nc
        self.ops = []

    enabled = True

    def add(self, eng, fn, reads=(), writes=(), dma=False):
        import os
        lim = int(os.environ.get("MK_NOPS", "0"))
        if self.enabled and (lim == 0 or len(self.ops) < lim):
            self.ops.append(_Op(eng, fn, reads, writes, dma))

    def pe(self, fn, reads=(), writes=()):
        self.add("pe", fn, reads, writes)

    def act(self, fn, reads=(), writes=()):
        self.add("act", fn, reads, writes)

    def dve(self, fn, reads=(), writes=()):
        self.add("dve", fn, reads, writes)

    def pool(self, fn, reads=(), writes=()):
        self.add("pool", fn, reads, writes)

    def dma(self, q, fn, reads=(), writes=()):
        self.add(q, fn, reads, writes, dma=True)

    def _analyze(self):
        ops = self.ops
        W = {}
        R = {}
        for idx, op in enumerate(ops):
            op.idx = idx
            deps = set()
            for r in op.reads:
                w = W.get(r)
                if w:
                    deps.update(w[0].values())
                    deps.update(w[1])
            for wkey in op.writes:
                rd = R.get(wkey)
                w = W.get(wkey)
                if rd and (rd[0] or rd[1]):
                    deps.update(rd[0].values())
                    deps.update(rd[1])
                    if w:
                        deps.update(w[0].values())
                        deps.update(w[1])
                    W[wkey] = ({}, [])
                    R[wkey] = ({}, [])
                else:
                    if w:
                        deps.update(w[0].values())
                        deps.update(w[1])
                w = W.setdefault(wkey, ({}, []))
                if op.dma:
                    w[1].append(idx)
                else:
                    w[0][op.eng] = idx
            for r in op.reads:
                rd = R.setdefault(r, ({}, []))
                if op.dma:
                    rd[1].append(idx)
                else:
                    rd[0][op.eng] = idx
            deps.discard(idx)
            best = {}
            dmas = []
            for d in deps:
                p = ops[d]
                if p.dma:
                    dmas.append(d)
                else:
                    if p.eng == op.eng and not op.dma and op.eng == "pe":
                        continue
                    if p.eng not in best or best[p.eng] < d:
                        best[p.eng] = d
            op.deps = tuple(best.values()) + tuple(dmas)
            for d in best.values():
                ops[d].signal = True
        last = {}
        for op in ops:
            if not op.dma:
                last[op.eng] = op
        for op in last.values():
            op.signal = True

    def emit(self, sems, dma_sems):
        nc = self.nc
        self._analyze()
        engobj = {"pe": nc.tensor, "act": nc.scalar, "dve": nc.vector, "pool": nc.gpsimd,
                  "sp": nc.sync}
        cnt = {e: 0 for e in COMPUTE}
        dcount = {q: 0 for q in dma_sems}
        waited = {}
        nwaits = 0
        for op in self.ops:
            e = engobj[op.eng]
            need = {}
            for d in op.deps:
                p = self.ops[d]
                if p.dma:
                    key = ("d", id(p.dsem))
                    sem, val = p.dsem, p.dval
                else:
                    key = ("c", p.eng)
                    sem, val = sems[p.eng], p.sig_idx
                if key not in need or need[key][1] < val:
                    need[key] = (sem, val)
            if op.dma:
                k = dcount[op.eng]
                dcount[op.eng] += 1
                pool = dma_sems[op.eng]
                op.dsem = pool[k % len(pool)]
                op.dval = 16 * (k // len(pool) + 1)
                if op.dval > 16:
                    key = ("d", id(op.dsem))
                    if key not in need or need[key][1] < op.dval - 16:
                        need[key] = (op.dsem, op.dval - 16)
            if os.environ.get("MK_DUMP"):
                print("OP", op.idx, op.eng, "dma" if op.dma else "", "R", op.reads, "W", op.writes, "deps",
                      [(self.ops[d].eng, d) for d in op.deps], "need", [(k[0], getattr(s_, "name", None), v) for k, (s_, v) in need.items()])
            for key, (sem, val) in need.items():
                wk = (op.eng, key)
                if waited.get(wk, 0) >= val:
                    continue
                waited[wk] = val
                e.wait_ge(sem, val)
                nwaits += 1
            ins = op.fn(e)
            if op.dma:
                ins.then_inc(op.dsem, 16)
            elif op.signal:
                cnt[op.eng] += 1
                op.sig_idx = cnt[op.eng]
                ins.then_inc(sems[op.eng], 1)
        for eng_ in COMPUTE:
            if cnt[eng_] > 0:
                nc.sync.wait_ge(sems[eng_], cnt[eng_])
        for q, pool in dma_sems.items():
            k = dcount[q]
            for i, s in enumerate(pool):
                uses = (k - i + len(pool) - 1) // len(pool) if k > i else 0
                if uses > 0:
                    nc.sync.wait_ge(s, 16 * uses)
        return dict(n_ops=len(self.ops), n_waits=nwaits, sig=cnt, dma=dcount)


def build(S):
    NG = S // G
    NT = S // 128
    nc = bass.Bass("TRN2", target_bir_lowering=False)

    def din(name, shape, dt=F32):
        return nc.dram_tensor(name, list(shape), dt, kind="ExternalInput").ap()

    x_d = din("x", [S, D])
    p_d = din("p", [S, PLE])
    w_in_d = din("w_in", [D, 6216])
    w_a_d = din("w_a", [512, D])
    w_b_d = din("w_b", [512, D])
    w_out_d = din("w_out", [D, D])
    w_gate_d = din("w_gate", [D, D])
    w_proj_d = din("w_proj", [PLE, D])
    gin_d = din("gin", [128, KC])
    gple_d = din("gple", [128, KC])
    gfin_d = din("gfin", [128, D])
    convw_d = din("convw", [128, 4, CONVK])
    convb_d = din("convb", [128, 4])
    lng_d = din("lng", [128, 4])
    lnb_d = din("lnb", [128, 4])
    d0t_d = din("d0t", [128, 8, 128])
    d1t_d = din("d1t", [128, 8, 128])
    b15_d = din("b15", [128, 8])
    ident_d = din("ident", [128, 128])
    pow2_d = din("pow2", [128, NIT + 2])
    out_d = nc.dram_tensor("out", [S, D], F32, kind="ExternalOutput").ap()

    wq_in = nc.dram_tensor("wq_in", [len(WBLOCKS), 128, 4096], BF16, kind="Internal").ap()
    wq_a = nc.dram_tensor("wq_a", [2, 128, 2048], BF16, kind="Internal").ap()
    wq_b = nc.dram_tensor("wq_b", [2, 128, 2048], BF16, kind="Internal").ap()
    wq_out = nc.dram_tensor("wq_out", [2, 128, 4096], BF16, kind="Internal").ap()
    wq_gate = nc.dram_tensor("wq_gate", [2, 128, 4096], BF16, kind="Internal").ap()
    wq_proj = nc.dram_tensor("wq_proj", [128, 2048], BF16, kind="Internal").ap()

    sc = Sched(nc)
    st = ExitStack()
    import os
    stop_at = os.environ.get("MK_STOP", "")

    def chk(tag):
        if stop_at and tag == stop_at:
            sc.enabled = False

    dbg_name = os.environ.get("MK_DBG", "")

    def dbg(name, ap_fn, keys, ncols):
        if dbg_name != name or not sc.enabled:
            return
        sc.dve(lambda e: e.tensor_copy(out=dbgt[:, 0:ncols], in_=ap_fn()), reads=keys, writes=["dbgt"])
        sc.dma("sp", lambda e: e.dma_start(out=out_d[0:128, 0:ncols], in_=dbgt[:, 0:ncols]), reads=["dbgt"])
        sc.enabled = False

    def sb(name, shape, dt):
        return st.enter_context(nc.sbuf_tensor("sb_" + name, list(shape), dt))

    kT = sb("kT", [128, 4, S], BF16)
    vaug = sb("vaug", [128, NT, 4, 160], BF16)
    kidx2 = sb("kidx2", [128, S], BF16)
    identb = sb("identb", [128, 128], BF16)
    d0t = sb("d0t", [128, 8, 128], F32)
    d1t = sb("d1t", [128, 8, 128], F32)
    b15 = sb("b15", [128, 8], F32)
    pow2 = sb("pow2", [128, NIT + 2], F32)
    gin = sb("gin", [128, KC], F32)
    gple = sb("gple", [128, KC], F32)
    gfin = sb("gfin", [128, D], F32)
    convw = sb("convw", [128, 4, CONVK], F32)
    convb = sb("convb", [128, 4], F32)
    lng = sb("lng", [128, 4], F32)
    lnb = sb("lnb", [128, 4], F32)

    big = sb("big", [128, 4096], F32)
    ptl = sb("ptl", [128, SUB, PLE], F32)
    pbf = sb("pbf", [128, SUB, PLE], BF16)
    pT = sb("pT", [128, 2, G], BF16)
    hT2 = [sb("hT0", [128, KC, G], BF16), sb("hT1", [128, KC, G], BF16)]
    wblk = [sb(f"wblk{i}", [128, 4096], BF16) for i in range(3)]
    wbr = sb("wbr", [128, 2048], BF16)
    qTe = sb("qTe", [128, 4, G], BF16)
    qTo = sb("qTo", [128, 4, G], BF16)
    szaT = sb("szaT", [128, 4, G], BF16)
    qidxTe = sb("qidxTe", [128, 4, G], BF16)
    qidxTo = sb("qidxTo", [128, 4, G], BF16)
    szbT = sb("szbT", [128, 4, G], BF16)
    uT = sb("uT", [128, 4, 30 + G], F32)
    gtmp = [sb(f"gtmp{i}", [128, G], BF16) for i in range(2)]
    wabs = sb("wabs", [128, SUB, 8], F32)
    wsgn = sb("wsgn", [128, SUB, 8], F32)
    stat = sb("stat", [128, 32], F32)
    Mq = sb("Mq", [128, 4096], BF16)
    MT = sb("MT", [128, NT, G], U8)
    xn = Mq[:, 0:D]
    PT = [sb(f"PT{i}", [128, G], BF16) for i in range(5)]
    tmpD = [sb(f"tmpD{i}", [128, 128], F32) for i in range(2)]
    hl = [[PT[0], PT[1]], [PT[2], PT[3]]]
    gatedT = sb("gatedT", [128, 4, G], BF16)
    gatedbT = sb("gatedbT", [128, 4, G], BF16)
    cacc = sb("cacc", [128, 4, G], F32)
    lnt_all = sb("lnt_all", [128, 5, G], F32)
    lnt = [lnt_all[:, i, :] for i in range(5)]
    Rb = [lnt_all[:, 0:2, :].rearrange("p a b -> p (a b)"), lnt_all[:, 2:4, :].rearrange("p a b -> p (a b)")]
    rden = sb("rden", [128, G], F32)
    neghalf = sb("neghalf", [128, 8], F32)
    sel_e = sb("sel_e", [128, 128], F32)
    sel_o = sb("sel_o", [128, 128], F32)
    oneslnb = sb("oneslnb", [128, 128], BF16)
    bcs = lnt[1]
    t1 = lnt[2]
    mab = [sb(f"mab{i}", [128, G], F32) for i in range(2)]
    mergedT = sb("mergedT", [128, KC, G], BF16)
    sqb = mergedT[:].rearrange("p a b -> p (a b)").bitcast(F32).rearrange("p (c t) -> p c t", c=4)
    bis = sb("bis", [128, NIT + 8], F32)
    dbgt = sb("dbgt", [128, 1024], F32) if os.environ.get("MK_DBG") else None
    prb = sb("prb", [128, 12], F32)
    nbis = sb("nbis", [128, NIT + 8], F32)

    ps = [st.enter_context(nc.psum_tensor(f"ps{i}", [128, 512], F32)) for i in range(8)]
    sems = {e: st.enter_context(nc.semaphore(f"s_{e}")) for e in COMPUTE}
    dsems = {q: [st.enter_context(nc.semaphore(f"d_{q}{i}")) for i in range(16)]
             for q in ("sp", "pool")}

    xs = big[:, 0:SUB * D].rearrange("p (s d) -> p s d", s=SUB)
    acc = big
    gsb = big[:, 2048:2560]
    tmpe = big[:, 2560:3072]
    ob = big[:, 3072:4096]

    def bigkeys(c0, c1):
        return [f"big{k}" for k in range(c0 // 512, (c1 + 511) // 512)]

    rot = {"w": 0, "a": 0}

    def ps_full(kind="w"):
        base = 4 if kind == "w" else 0
        rot[kind] = (rot[kind] + 1) & ~1
        b = base + (rot[kind] // 2) % 4
        rot[kind] += 2
        if os.environ.get("MK_PSDBG"):
            import traceback
            fr = traceback.extract_stack(limit=3)
            print("PSALLOC", b, len(sc.ops), [f.lineno for f in fr[:-1]])
        return ps[b], [f"ps{b}a", f"ps{b}b"]

    att_rot = [0]

    def ps_att():
        b = 2 + att_rot[0] % 6
        att_rot[0] += 1
        return ps[b][:, 0:256], [f"ps{b}a", f"ps{b}b"]

    half_owner = {}

    def ps_half(kind="w"):
        t, keys = ps_full(kind)
        ap = t[:, 0:256]
        half_owner[id(ap)] = t
        return ap, keys

    def pe_fence(pst, keys):
        sc.pe(lambda e: e.matmul(pst[:, 448:449], lhsT=identb[:], rhs=identb[:, 0:1], start=True, stop=True),
              reads=["identb"], writes=keys)

    cp_rr = [int(os.environ.get("MK_CP", "0"))]

    def copy_any(out, in_, reads, writes):
        cp_rr[0] ^= 1
        if cp_rr[0]:
            sc.act(lambda e: e.activation(out=out, in_=in_, func=AF.Identity), reads, writes)
        else:
            sc.dve(lambda e: e.tensor_copy(out=out, in_=in_), reads, writes)

    def ld(q, dst, src, key):
        sc.dma(q, lambda e: e.dma_start(out=dst, in_=src), writes=[key])

    ld("sp", big[:, 0:128], ident_d, "big0")
    sc.act(lambda e: e.activation(out=identb[:], in_=big[:, 0:128], func=AF.Identity),
           reads=["big0"], writes=["identb"])
    ld("sp", d0t[:], d0t_d, "d0t")
    ld("sp", d1t[:], d1t_d, "d1t")
    ld("sp", b15[:], b15_d, "b15")
    ld("sp", pow2[:], pow2_d, "pow2")
    ld("sp", gin[:], gin_d, "gin")
    ld("sp", gple[:], gple_d, "gple")
    ld("sp", gfin[:], gfin_d, "gfin")
    ld("sp", convw[:], convw_d, "convw")
    ld("sp", convb[:], convb_d, "convb")
    ld("sp", lng[:], lng_d, "lng")
    ld("sp", lnb[:], lnb_d, "lnb")
    sc.pool(lambda e: e.memset(neghalf[:], -0.5), writes=["neghalf"])
    sc.pool(lambda e: e.memset(qTe[:], 0.0), writes=["qTe"])
    sc.pool(lambda e: e.memset(qTo[:], 0.0), writes=["qTo"])
    sc.pool(lambda e: e.memset(qidxTe[:], 0.0), writes=["qidxTe"])
    sc.pool(lambda e: e.memset(qidxTo[:], 0.0), writes=["qidxTo"])
    sc.pool(lambda e: e.memset(oneslnb[:], 1.0 / 512.0), writes=["oneslnb"])
    sc.pool(lambda e: e.memset(vaug[:, :, :, 64:96], 0.0), writes=["vaug_c"])
    sc.pool(lambda e: e.memset(vaug[:, :, :, 64:65], 1.0), writes=["vaug_c"])
    sc.pool(lambda e: e.memset(uT[:, :, 0:30], 0.0), writes=["uTh"])
    sc.pool(lambda e: e.memset(rden[:], 0.0), writes=["rden0", "rden1"])
    sc.pool(lambda e: e.memset(sel_e[:], 0.0), writes=["sel"])
    sc.pool(lambda e: e.memset(sel_o[:], 0.0), writes=["sel"])
    sc.pool(lambda e: e.memset(sel_e[64:65, :], 1.0), writes=["sel"])
    sc.pool(lambda e: e.memset(sel_o[32:33, :], 1.0), writes=["sel"])

    chk('w')
    w_in_v = w_in_d.rearrange("(k p) c -> p k c", p=128)
    stg = [0]

    def stage_weight(parts, dst):
        i = stg[0] % 3
        stg[0] += 1
        for (dv, srcap) in parts:
            sc.dma("pool", lambda e, dv=dv, srcap=srcap, i=i: e.dma_start(out=dv(wblk[i]), in_=srcap),
                   writes=[f"wblk{i}"])
        n = dst.shape[-1]
        sc.dma("sp", lambda e, i=i, dst=dst, n=n: e.dma_start(out=dst, in_=wblk[i][:, 0:n]),
               reads=[f"wblk{i}"], writes=["wq"])

    v8 = lambda t: t[:].rearrange("p (k c) -> p k c", k=8)
    v4 = lambda t: t[:].rearrange("p (k c) -> p k c", k=4)
    v2 = lambda t: t[:, 0:2048].rearrange("p (k c) -> p k c", k=2)
    for bi, name in enumerate(WBLOCKS):
        if name == "kiwi":
            parts = [
                (lambda t: v8(t)[:, :, 0:64], w_in_v[:, :, 2560:2624]),
                (lambda t: v8(t)[:, :, 64:128], w_in_v[:, :, 2560:2624]),
                (lambda t: v8(t)[:, :, 128:136], w_in_v[:, :, 2624:2632]),
            ]
        else:
            c0 = WB_COL[name]
            parts = [(lambda t: v8(t), w_in_v[:, :, c0:c0 + 512])]
        stage_weight(parts, wq_in[bi])
    v4h = lambda t: t[:, 0:2048].rearrange("p (k c) -> p k c", k=4)
    for hf in range(2):
        stage_weight([(v4h, w_a_d.rearrange("(k p) c -> p k c", p=128)[:, :, hf * 512:(hf + 1) * 512])], wq_a[hf])
        stage_weight([(v4h, w_b_d.rearrange("(k p) c -> p k c", p=128)[:, :, hf * 512:(hf + 1) * 512])], wq_b[hf])
    for hf in range(2):
        stage_weight([(v8, w_out_d.rearrange("(k p) c -> p k c", p=128)[:, :, hf * 512:(hf + 1) * 512])], wq_out[hf])
        stage_weight([(v8, w_gate_d.rearrange("(k p) c -> p k c", p=128)[:, :, hf * 512:(hf + 1) * 512])], wq_gate[hf])
    stage_weight([(v2, w_proj_d.rearrange("(k p) c -> p k c", p=128))], wq_proj)

    wl = [0]

    def load_w(src, n=4096):
        i = wl[0] % 3
        wl[0] += 1
        sc.dma("sp", lambda e, i=i, src=src, n=n: e.dma_start(out=wblk[i][:, 0:n], in_=src),
               reads=["wq"], writes=[f"wblk{i}"])
        return wblk[i], f"wblk{i}"

    def rms_rstd(col0):
        for s in range(SUB):
            sc.act(lambda e, s=s: e.activation(out=Mq[:, D:2 * D], in_=xs[:, s, :], func=AF.Square,
                                               accum_out=stat[:, 8 + s:9 + s]),
                   reads=bigkeys(s * D, (s + 1) * D), writes=["Mq", f"st{8 + s}"])
        sc.dve(lambda e: e.tensor_scalar(out=stat[:, 10:10 + SUB], in0=stat[:, 8:8 + SUB], scalar1=1.0 / D,
                                         scalar2=EPS, op0=ALU.mult, op1=ALU.add),
               reads=[f"st{8 + s}" for s in range(SUB)], writes=["st10"])
        sc.pool(lambda e: e.tensor_tensor(out=stat[:, col0:col0 + SUB], in0=stat[:, 10:10 + SUB], in1=neghalf[:, 0:SUB],
                                          op=ALU.pow), reads=["st10", "neghalf"], writes=[f"rstd{col0}"])
        return f"rstd{col0}"

    def norm_transpose(rkey, col0, gvec, gkey, dstT, dkey):
        for s in range(SUB):
            sc.dve(lambda e, s=s: e.tensor_scalar(out=xn, in0=xs[:, s, :], scalar1=stat[:, col0 + s:col0 + s + 1],
                                                  scalar2=None, op0=ALU.mult),
                   reads=bigkeys(s * D, (s + 1) * D) + [rkey], writes=["Mq"])
            pt_, pk = ps_full()
            ptb = pt_[:].bitcast(BF16)
            for kc in range(KC):
                sc.pe(lambda e, kc=kc, ptb=ptb: e.transpose(out=ptb[:, kc * 128:(kc + 1) * 128],
                                                           in_=xn[:, kc * 128:(kc + 1) * 128], identity=identb[:]),
                      reads=["Mq", "identb"], writes=pk)
            sc.dve(lambda e, s=s, ptb=ptb: e.tensor_tensor(
                out=dstT[:, :, s * 128:(s + 1) * 128], in0=ptb.rearrange("p (k t) -> p k t", k=KC),
                in1=gvec[:, :].unsqueeze(2).to_broadcast([128, KC, 128]), op=ALU.mult),
                reads=pk + [gkey], writes=[dkey])

    def head(g):
        t0 = g * G
        hT = hT2[g % 2]
        hTk = f"hT{g % 2}"
        stg = Mq[:, 2048:4096].bitcast(F32)
        for s in range(SUB):
            sc.dma("pool", lambda e, t0=t0, s=s: e.dma_start(out=stg, in_=x_d[t0 + s * 128:t0 + (s + 1) * 128, :]),
                   writes=["MqH"])
            sc.act(lambda e: e.activation(out=Mq[:, D:2 * D], in_=stg, func=AF.Square, accum_out=stat[:, 16:17]),
                   reads=["MqH"], writes=["Mq", "st16"])
            sc.dve(lambda e: e.tensor_scalar(out=stat[:, 17:18], in0=stat[:, 16:17], scalar1=1.0 / D, scalar2=EPS,
                                             op0=ALU.mult, op1=ALU.add), reads=["st16"], writes=["st17"])
            sc.pool(lambda e: e.tensor_tensor(out=stat[:, 18:19], in0=stat[:, 17:18], in1=neghalf[:, 0:1], op=ALU.pow),
                    reads=["st17", "neghalf"], writes=["st18"])
            sc.dve(lambda e: e.tensor_scalar(out=xn, in0=stg, scalar1=stat[:, 18:19], scalar2=None, op0=ALU.mult),
                   reads=["MqH", "st18"], writes=["Mq"])
            pt_, pk = ps_full()
            ptb = pt_[:].bitcast(BF16)
            for kc in range(KC):
                sc.pe(lambda e, kc=kc, ptb=ptb: e.transpose(out=ptb[:, kc * 128:(kc + 1) * 128],
                                                           in_=xn[:, kc * 128:(kc + 1) * 128], identity=identb[:]),
                      reads=["Mq", "identb"], writes=pk)
            sc.dve(lambda e, s=s, ptb=ptb, hT=hT: e.tensor_tensor(
                out=hT[:, :, s * 128:(s + 1) * 128], in0=ptb.rearrange("p (k t) -> p k t", k=KC),
                in1=gin[:, :].unsqueeze(2).to_broadcast([128, KC, 128]), op=ALU.mult),
                reads=pk + ["gin"], writes=[hTk])
            yield
        def fm_block(wt, wk, ncc, epi, col_of=lambda cc: cc * 128):
            wv = v8(wt)
            chk('mm')
            for cc in range(ncc):
                pt_, pk = ps_half()
                c0 = col_of(cc)
                for kc in range(KC):
                    sc.pe(lambda e, kc=kc, c0=c0, pt_=pt_, wv=wv: e.matmul(
                        pt_, lhsT=wv[:, kc, c0:c0 + 128], rhs=hT[:, kc, :], start=(kc == 0), stop=(kc == KC - 1)),
                        reads=[wk, hTk], writes=pk)
                chk('epi')
                epi(cc, pt_, pk)

        def epi_copy(dst_fn, dkey):
            def f(cc, pt_, pk):
                copy_any(dst_fn(cc), pt_, pk, [dkey])
            return f

        def epi_act(dst_fn, dkey, func):
            def f(cc, pt_, pk):
                sc.act(lambda e: e.activation(out=dst_fn(cc), in_=pt_, func=func), pk, [dkey])
            return f

        for bi, name in enumerate(WBLOCKS[:9]):
            chk(f'c{g}_{name}')
            wt, wk = load_w(wq_in[bi])
            if name == "q":
                def epi_q(cc, pt_, pk):
                    copy_any(qTe[0:64, cc, :], pt_[0:64, :], pk, ["qTe"])
                    copy_any(qTo[64:128, cc, :], pt_[64:128, :], pk, ["qTo"])
                fm_block(wt, wk, 4, epi_q)
            elif name == "k":
                fm_block(wt, wk, 4, epi_copy(lambda cc, t0=t0: kT[:, cc, t0:t0 + G], f"kT{g}"))
            elif name == "v":
                wv = v8(wt)
                for s in range(SUB):
                    pt_, pk = ps_full()
                    for kc in range(KC):
                        sc.pe(lambda e, kc=kc, s=s, pt_=pt_, wv=wv: e.matmul(
                            pt_[:], lhsT=hT[:, kc, s * 128:(s + 1) * 128], rhs=wv[:, kc, :],
                            start=(kc == 0), stop=(kc == KC - 1)), reads=[wk, hTk], writes=pk)
                    j = g * SUB + s
                    pv = pt_[:].rearrange("p (c e d) -> p c e d", c=4, e=2)
                    copy_any(vaug[:, j, :, 0:64], pv[:, :, 0, :], pk, [f"vaug{g}"])
                    copy_any(vaug[:, j, :, 96:160], pv[:, :, 1, :], pk, [f"vaug{g}"])
            elif name == "za":
                fm_block(wt, wk, 4, epi_act(lambda cc: szaT[:, cc, :], "szaT", AF.Silu))
            elif name == "qi":
                def epi_qi(cc, pt_, pk):
                    copy_any(qidxTe[0:64, cc, :], pt_[0:64, :], pk, ["qidxTe"])
                    copy_any(qidxTo[64:128, cc, :], pt_[64:128, :], pk, ["qidxTo"])
                fm_block(wt, wk, 4, epi_qi)
            elif name == "kiwi":
                fm_block(wt, wk, 1, epi_copy(lambda cc, t0=t0: kidx2[:, t0:t0 + G], f"kidx{g}"))
                wv = v8(wt)
                for s in range(SUB):
                    pt_, pk = ps_half()
                    for kc in range(KC):
                        sc.pe(lambda e, kc=kc, s=s, pt_=pt_, wv=wv: e.matmul(
                            pt_[:, 0:8], lhsT=hT[:, kc, s * 128:(s + 1) * 128], rhs=wv[:, kc, 128:136],
                            start=(kc == 0), stop=(kc == KC - 1)), reads=[wk, hTk], writes=pk)
                    sc.act(lambda e, s=s, pt_=pt_: e.activation(out=wabs[:, s, :], in_=pt_[:, 0:8], func=AF.Abs,
                                                                scale=IDX_SCALE), reads=pk, writes=["wabs"])
                    sc.act(lambda e, s=s, pt_=pt_: e.activation(out=wsgn[:, s, :], in_=pt_[:, 0:8], func=AF.Sign),
                           reads=pk, writes=["wsgn"])
            elif name == "glua":
                fm_block(wt, wk, 4, epi_copy(lambda cc: uT[:, cc, 30:30 + G], "uT"))
            elif name == "glub":
                def epi_glu(cc, pt_, pk):
                    sc.act(lambda e: e.activation(out=lnt[0][:], in_=pt_, func=AF.Sigmoid), pk, ["lnt0"])
                    sc.dve(lambda e: e.tensor_tensor(out=uT[:, cc, 30:30 + G], in0=uT[:, cc, 30:30 + G], in1=lnt[0][:],
                                                     op=ALU.mult), ["uT", "lnt0"], ["uT"])
                fm_block(wt, wk, 4, epi_glu)
            elif name == "zb":
                fm_block(wt, wk, 4, epi_act(lambda cc: szbT[:, cc, :], "szbT", AF.Silu))
            yield

    def middle(g):
        t0 = g * G
        def conv_gen(ccs):
            for cc in ccs:
                sc.dve(lambda e, cc=cc: e.tensor_scalar(out=cacc[:, cc, :], in0=uT[:, cc, 0:G], scalar1=convw[:, cc, 0:1],
                                                        scalar2=convb[:, cc:cc + 1], op0=ALU.mult, op1=ALU.add),
                       reads=["uT", "uTh", "convw", "convb"], writes=[f"cacc{cc}"])
            for jj in range(1, CONVK):
                for cc in ccs:
                    sc.dve(lambda e, cc=cc, jj=jj: e.scalar_tensor_tensor(
                        out=cacc[:, cc, :], in0=uT[:, cc, jj:jj + G], scalar=convw[:, cc, jj:jj + 1], in1=cacc[:, cc, :],
                        op0=ALU.mult, op1=ALU.add), reads=["uT", "uTh", "convw", f"cacc{cc}"], writes=[f"cacc{cc}"])
                yield

        for s in range(SUB):
            i = g * SUB + s
            n = 128 * (i + 1)
            nkb = (n + 511) // 512
            for kb in range(nkb):
                cols = min(512, n - kb * 512)
                kkeys = sorted({f"kidx{(kb * 512 + o) // G}" for o in range(0, cols, 128)})
                for h in range(8):
                    c, eo = h // 2, h % 2
                    pt_, pk = ps_full()
                    qsel = qidxTe if eo == 0 else qidxTo
                    sc.pe(lambda e, c=c, qsel=qsel, s=s, kb=kb, cols=cols, pt_=pt_: e.matmul(
                        pt_[:, 0:cols], lhsT=qsel[:, c, s * 128:(s + 1) * 128],
                        rhs=kidx2[:, kb * 512:kb * 512 + cols], start=True, stop=True),
                        reads=["qidxTe", "qidxTo"] + kkeys, writes=pk)
                    rb = Rb[h % 2]
                    rk_ = f"lnt{2 * (h % 2)}"
                    rk2_ = f"lnt{2 * (h % 2) + 1}"
                    sc.act(lambda e, s=s, h=h, cols=cols, pt_=pt_, rb=rb: e.activation(
                        out=rb[:, 0:cols], in_=pt_[:, 0:cols], func=AF.Relu, scale=wabs[:, s, h:h + 1]),
                        reads=pk + ["wabs"], writes=[rk_, rk2_])
                    ak = bigkeys(kb * 512, kb * 512 + cols)
                    if h == 0:
                        sc.dve(lambda e, s=s, kb=kb, cols=cols, rb=rb: e.tensor_scalar(
                            out=acc[:, kb * 512:kb * 512 + cols], in0=rb[:, 0:cols], scalar1=wsgn[:, s, 0:1],
                            scalar2=None, op0=ALU.mult), reads=[rk_, rk2_, "wsgn"], writes=ak)
                    else:
                        sc.dve(lambda e, s=s, h=h, kb=kb, cols=cols, rb=rb: e.scalar_tensor_tensor(
                            out=acc[:, kb * 512:kb * 512 + cols], in0=rb[:, 0:cols], scalar=wsgn[:, s, h:h + 1],
                            in1=acc[:, kb * 512:kb * 512 + cols], op0=ALU.mult, op1=ALU.add),
                            reads=[rk_, rk2_, "wsgn"] + ak, writes=ak)
            akall = bigkeys(0, n)
            sc.dve(lambda e, n=n: e.tensor_reduce(out=prb[:, 3:4], in_=acc[:, 0:n], axis=AX.X, op=ALU.max,
                                                  apply_absolute_value=True), reads=akall, writes=["prb3"])
            sc.dve(lambda e: e.tensor_scalar(out=prb[:, 3:4], in0=prb[:, 3:4], scalar1=2.0, scalar2=2.0,
                                             op0=ALU.mult, op1=ALU.add), reads=["prb3"], writes=["prb3"])
            sc.dve(lambda e: e.tensor_scalar(out=bis[:, 0:NIT + 2], in0=pow2[:], scalar1=prb[:, 3:4], scalar2=None,
                                             op0=ALU.mult), reads=["prb3", "pow2"], writes=["bis"])
            sc.dve(lambda e: e.tensor_scalar(out=nbis[:, 0:NIT + 2], in0=pow2[:], scalar1=prb[:, 3:4], scalar2=-1.0,
                                             op0=ALU.mult, op1=ALU.mult), reads=["prb3", "pow2"], writes=["nbis"])
            sc.dve(lambda e: e.memset(prb[:, 0:1], 0.0), writes=["prb0"])
            sc.dve(lambda e, n=n: e.memset(prb[:, 4:5], float(n - 2 * TOPK + 1)), writes=["prb4"])
            sc.dve(lambda e, n=n: e.memset(acc[0:64, n - 64:n], NEG), reads=["prb3"], writes=bigkeys(n - 64, n))
            cg = conv_gen([2 * s, 2 * s + 1])
            split = n >= 1024
            nA = (3 * n // 4) // 128 * 128 if split else n
            nB = n - nA
            junkB = lnt_all[:].rearrange("p a b -> p (a b)").bitcast(BF16)
            lkeys = ["lnt0", "lnt1", "lnt2", "lnt3"]
            for it in range(1, NIT + 1):
                pi, po = (it - 1) % 2, it % 2
                sc.act(lambda e, nA=nA, pi=pi: e.activation(
                    out=Mq[:, 0:nA], in_=acc[:, 0:nA], func=AF.Sign, bias=prb[:, pi:pi + 1], scale=1.0,
                    accum_out=prb[:, 2:3]), reads=akall + [f"prb{pi}"], writes=["Mq", "MqH", "prb2"])
                if split:
                    sc.dve(lambda e, pi=pi: e.tensor_scalar(out=prb[:, 8:9], in0=prb[:, pi:pi + 1], scalar1=-1.0,
                                                            scalar2=None, op0=ALU.mult),
                           reads=[f"prb{pi}"], writes=["prb8"])
                    sc.dve(lambda e, nA=nA, n=n, nB=nB: e.tensor_scalar(
                        out=junkB[:, 0:nB], in0=acc[:, nA:n], scalar1=prb[:, 8:9], scalar2=None,
                        op0=ALU.is_ge, op1=ALU.add, accum_out=prb[:, 6:7]),
                        reads=akall + ["prb8"], writes=lkeys + ["prb6"])
                    sc.dve(lambda e, nA=nA: e.tensor_scalar(out=prb[:, 7:8], in0=prb[:, 6:7], scalar1=2.0,
                                                            scalar2=float(nA - 2 * TOPK + 1), op0=ALU.mult, op1=ALU.add),
                           reads=["prb6"], writes=["prb7"])
                    bcol, bkey = 7, "prb7"
                else:
                    bcol, bkey = 4, "prb4"
                sc.act(lambda e, bcol=bcol: e.activation(out=prb[:, 3:4], in_=prb[:, 2:3], func=AF.Sign,
                                                         bias=prb[:, bcol:bcol + 1], scale=1.0),
                       reads=["prb2", bkey], writes=["prb3"])
                sc.act(lambda e, it=it, pi=pi, po=po: e.activation(
                    out=prb[:, po:po + 1], in_=prb[:, 3:4], func=AF.Identity, scale=nbis[:, it + 1:it + 2],
                    bias=prb[:, pi:pi + 1]), reads=["prb3", "nbis", f"prb{pi}"], writes=[f"prb{po}"])
                for _ in range(2):
                    next(cg, None)
            for _ in cg:
                pass
            pf = NIT % 2
            sc.dve(lambda e, pf=pf: e.tensor_scalar(out=prb[:, 5:6], in0=prb[:, pf:pf + 1], scalar1=-1.0,
                                                    scalar2=bis[:, NIT + 1:NIT + 2], op0=ALU.mult, op1=ALU.subtract),
                   reads=[f"prb{pf}", "bis"], writes=["prb5"])
            sc.dve(lambda e, n=n: e.tensor_scalar(out=Mq[:, 0:n], in0=acc[:, 0:n], scalar1=prb[:, 5:6],
                                                  scalar2=None, op0=ALU.is_ge),
                   reads=akall + ["prb5"], writes=["Mq", "MqH"])
            for jb0 in range(0, i + 1, 8):
                nb = min(8, i + 1 - jb0)
                pt_, pk = ps_full()
                ptb = pt_[:].bitcast(BF16)
                for k in range(nb):
                    sc.pe(lambda e, k=k, jb0=jb0, ptb=ptb: e.transpose(
                        out=ptb[:, k * 128:(k + 1) * 128], in_=Mq[:, (jb0 + k) * 128:(jb0 + k + 1) * 128],
                        identity=identb[:]), reads=["Mq", "MqH", "identb"], writes=pk)
                copy_any(MT[:, jb0:jb0 + nb, s * 128:(s + 1) * 128],
                         ptb[:, 0:nb * 128].rearrange("p (k t) -> p k t", k=nb), pk, ["MT"])

        dbg('acc', lambda: acc[:, 0:256], bigkeys(0, 256), 256)
        dbg('Mq', lambda: Mq[:, 0:256], ['Mq'], 256)
        dbg('MT', lambda: MT[:, 0:2, :].rearrange('p j t -> p (j t)'), ['MT'], 2 * G)
        dbg('prb', lambda: prb[:, 0:4], ['prb0', 'prb1', 'prb2', 'prb3'], 4)
        chk(f'e{g}')
        jlast = g * SUB + SUB - 1
        ptr = [0]
        LA = 5
        HPP = 2
        for hp in range(8 // HPP):
            items = [(j, hh) for j in range(jlast + 1) for hh in range(HPP)]
            ltile = {}

            def emit_qk(it, hp=hp):
                j, hh = it
                m = j - g * SUB
                q0 = max(m, 0) * 128
                Tq = G - q0
                jg = j // SUB
                h = HPP * hp + hh
                c, eo = h // 2, h % 2
                lt, lk = ps_att()
                qsel = qTe if eo == 0 else qTo
                sc.pe(lambda e, c=c, qsel=qsel, j=j, q0=q0, Tq=Tq, lt=lt: e.matmul(
                    lt[:, 0:Tq], lhsT=kT[:, c, j * 128:(j + 1) * 128],
                    rhs=qsel[:, c, q0:G], start=True, stop=True),
                    reads=[f"kT{jg}", "qTe", "qTo"], writes=lk)
                ltile[it] = (lt, lk)

            def emit_rest(it, hp=hp, jlast=jlast):
                j, hh = it
                lt, lk = ltile.pop(it)
                m = j - g * SUB
                q0 = max(m, 0) * 128
                Tq = G - q0
                jg = j // SUB
                h = HPP * hp + hh
                c, eo = h // 2, h % 2
                pti = ptr[0] % 5
                ptr[0] += 1
                P = PT[pti]
                pkey = f"PT{pti}"
                ib0 = max(m, 0)
                far0 = max(m + 2, 0)
                for ib in range(ib0, min(far0, SUB)):
                    dsel = d0t if ib == m else d1t
                    dk = "d0t" if ib == m else "d1t"
                    lo = (ib - ib0) * 128
                    td = tmpD[ib % 2]
                    tk = f"tmpD{ib % 2}"
                    sc.dve(lambda e, lo=lo, h=h, lt=lt, td=td, dsel=dsel: e.scalar_tensor_tensor(
                        out=td[:], in0=lt[:, lo:lo + 128], scalar=0.125, in1=dsel[:, h, :],
                        op0=ALU.mult, op1=ALU.add), reads=lk + [dk], writes=[tk])
                    sc.act(lambda e, lo=lo, P=P, td=td: e.activation(out=P[:, lo:lo + 128], in_=td[:], func=AF.Exp),
                           reads=[tk], writes=[pkey])
                if far0 < SUB:
                    lo = (far0 - ib0) * 128
                    sc.act(lambda e, lo=lo, Tq=Tq, h=h, P=P, lt=lt: e.activation(
                        out=P[:, lo:Tq], in_=lt[:, lo:Tq], func=AF.Exp, bias=b15[:, h:h + 1], scale=0.125),
                        reads=lk + ["b15"], writes=[pkey])
                mul_eng = sc.pool if (ptr[0] % 4 == 0) else sc.dve
                mul_eng(lambda e, j=j, q0=q0, Tq=Tq, P=P: e.tensor_tensor(
                    out=P[:, 0:Tq], in0=P[:, 0:Tq], in1=MT[:, j, q0:G], op=ALU.mult),
                    reads=[pkey, "MT"], writes=[pkey])
                ab = ps[hh]
                akey = [f"ps{hh}a", f"ps{hh}b"]
                if eo == 0:
                    sc.pe(lambda e, j=j, c=c, q0=q0, Tq=Tq, P=P, ab=ab, jlast=jlast: e.matmul(
                        ab[0:65, q0:G], lhsT=vaug[:, j, c, 0:65], rhs=P[:, 0:Tq],
                        start=(j == 0), stop=(j == jlast)),
                        reads=[pkey, f"vaug{jg}", "vaug_c"], writes=akey)
                else:
                    sc.pe(lambda e, j=j, c=c, q0=q0, Tq=Tq, P=P, ab=ab, jlast=jlast: e.matmul(
                        ab[:, q0:G], lhsT=vaug[:, j, c, 32:160], rhs=P[:, 0:Tq],
                        start=(j == 0), stop=(j == jlast)),
                        reads=[pkey, f"vaug{jg}", "vaug_c"], writes=akey)

            for it in items[:LA]:
                emit_qk(it)
            for ii, it in enumerate(items):
                emit_rest(it)
                if ii + LA < len(items):
                    emit_qk(items[ii + LA])
            for hh in range(HPP):
                h = HPP * hp + hh
                c, eo = h // 2, h % 2
                ab = ps[hh]
                akey = [f"ps{hh}a", f"ps{hh}b"]
                dr = 64 if eo == 0 else 32
                r0 = 0 if eo == 0 else 64
                hk = f"{eo}"
                sc.dve(lambda e, dr=dr, ab=ab: e.reciprocal(out=rden[dr:dr + 1, :], in_=ab[dr:dr + 1, 0:G]),
                       reads=akey, writes=[f"rden{hk}"])
                bt, bk = ps_half()
                selm = sel_e if eo == 0 else sel_o
                sc.pe(lambda e, bt=bt, selm=selm: e.matmul(bt, lhsT=selm[:], rhs=rden[:], start=True, stop=True),
                      reads=["rden0", "rden1", "sel"], writes=bk)
                sc.dve(lambda e, r0=r0, bt=bt: e.tensor_copy(out=bcs[r0:r0 + 64, :], in_=bt[r0:r0 + 64, :]),
                       reads=bk, writes=["lnt1"])
                sc.dve(lambda e, r0=r0, ab=ab: e.tensor_tensor(out=t1[r0:r0 + 64, :], in0=ab[r0:r0 + 64, 0:G],
                                                               in1=bcs[r0:r0 + 64, :], op=ALU.mult),
                       reads=akey + ["lnt1"], writes=["lnt2"])
                sc.dve(lambda e, r0=r0, c=c: e.tensor_tensor(out=gatedT[r0:r0 + 64, c, :], in0=t1[r0:r0 + 64, :],
                                                             in1=szaT[r0:r0 + 64, c, :], op=ALU.mult),
                       reads=["lnt2", "szaT"], writes=["gatedT"])

        dbg('gatedT', lambda: gatedT[:].rearrange('p k t -> p (k t)'), ['gatedT'], 1024)
        dbg('ps0pre', lambda: ps[0][:, 0:256], ['ps0a', 'ps0b'], 256)
        chk(f'f{g}')
    def tail(g):
        t0 = g * G
        hT = hT2[g % 2]
        hTk = f"hT{g % 2}"
        sc.dma("pool", lambda e, t0=t0: e.dma_start(out=ptl[:], in_=p_d[t0:t0 + G, :].rearrange("(s p) d -> p s d", p=128)),
               writes=["ptl"])
        if os.environ.get("MK_DBG") == "cacc":
            sc.dma("sp", lambda e: e.dma_start(out=out_d[0:128, :], in_=cacc[:].rearrange("p c t -> p (c t)")),
                   reads=[f"cacc{c_}" for c_ in range(4)])
        if os.environ.get("MK_DBG") == "uT":
            sc.dma("sp", lambda e: e.dma_start(out=out_d[0:128, :], in_=uT[:, :, 0:256].rearrange("p c t -> p (c t)")),
                   reads=["uT", "uTh"])
        chk('fconv')
        sc.pool(lambda e: e.tensor_copy(out=uT[:, :, 0:30], in_=uT[:, :, G:G + 30]), reads=["uT"], writes=["uTh"])
        chk('fhalo')
        def stat_mm(dst_t, dst_k, src_fn, skeys_fn):
            for cc in range(4):
                hi, lo = hl[0][cc % 2], hl[1][cc % 2]
                hk_, lk_ = f"PT{cc % 2}", f"PT{2 + cc % 2}"
                sc.dve(lambda e, cc=cc, hi=hi: e.tensor_copy(out=hi[:], in_=src_fn(cc)),
                       reads=skeys_fn(cc), writes=[hk_])
                sc.dve(lambda e, cc=cc, hi=hi, lo=lo: e.tensor_tensor(out=lo[:], in0=src_fn(cc), in1=hi[:], op=ALU.subtract),
                       reads=skeys_fn(cc) + [hk_], writes=[lk_])
                sc.pe(lambda e, cc=cc, hi=hi: e.matmul(dst_t, lhsT=oneslnb[:], rhs=hi[:], start=(cc == 0), stop=False),
                      reads=[hk_, "oneslnb"], writes=dst_k)
                if cc == 0:
                    dbg('mm1', lambda: dst_t, dst_k, G)
                sc.pe(lambda e, cc=cc, lo=lo: e.matmul(dst_t, lhsT=oneslnb[:], rhs=lo[:], start=False, stop=(cc == 3)),
                      reads=[lk_, "oneslnb"], writes=dst_k)
                if cc == 0:
                    dbg('mm2', lambda: dst_t, dst_k, G)
                    dbg('lo0', lambda: lo[:], [lk_], G)
                if dbg_name == "mmall" and sc.enabled and dst_t is mean_t:
                    for half_, src_ in ((0, hi), (1, lo)):
                        pass
                    k_ = cc
                    sc.dve(lambda e: e.tensor_copy(out=dbgt[:, 0:G], in_=dst_t), reads=dst_k, writes=["dbgt"])
                    sc.dve(lambda e, hi=hi: e.tensor_copy(out=dbgt[:, G:2 * G], in_=hi[:]), reads=[hk_], writes=["dbgt"])
                    sc.dve(lambda e, lo=lo: e.tensor_copy(out=dbgt[:, 2 * G:3 * G], in_=lo[:]), reads=[lk_], writes=["dbgt"])
                    sc.dma("sp", lambda e, k_=k_: e.dma_start(out=out_d[k_ * 128:(k_ + 1) * 128, 0:3 * G], in_=dbgt[:, 0:3 * G]),
                           reads=["dbgt"])
                    if cc == 3:
                        sc.enabled = False

        mean_t, mean_k = ps_half(os.environ.get('MK_MB', 'a'))
        stat_mm(mean_t, mean_k, lambda cc: cacc[:, cc, :], lambda cc: [f"cacc{cc}"])
        dbg('onesb', lambda: oneslnb[:], ['oneslnb'], 128)
        dbg('meanps', lambda: mean_t, mean_k, G)
        chk('fmean')
        ex2_t, ex2_k = ps_half(os.environ.get('MK_MB', 'a'))
        for cc in range(4):
            sc.act(lambda e, cc=cc: e.activation(out=sqb[:, cc, :], in_=cacc[:, cc, :], func=AF.Square),
                   reads=[f"cacc{cc}"], writes=[f"mg{2 * cc}", f"mg{2 * cc + 1}"])
        stat_mm(ex2_t, ex2_k, lambda cc: sqb[:, cc, :], lambda cc: [f"mg{2 * cc}", f"mg{2 * cc + 1}"])
        chk('fex2')
        if os.environ.get("MK_X") == "1":
            sc.dve(lambda e: e.tensor_copy(out=lnt[2][:], in_=lnt[4][:]), reads=mean_k, writes=["lnt2"])
        elif os.environ.get("MK_X") == "2":
            sc.dve(lambda e: e.tensor_copy(out=mab[0][:], in_=mean_t), reads=mean_k, writes=["lnt2"])
        else:
            sc.dve(lambda e, mean_t=mean_t: e.tensor_copy(out=lnt[2][:], in_=mean_t), reads=mean_k, writes=["lnt2"])
        sc.dve(lambda e: e.tensor_tensor(out=lnt[3][:], in0=lnt[2][:], in1=lnt[2][:], op=ALU.mult),
               reads=["lnt2"], writes=["lnt3"])
        sc.dve(lambda e, ex2_t=ex2_t: e.tensor_tensor(out=lnt[3][:], in0=ex2_t, in1=lnt[3][:], op=ALU.subtract),
               reads=ex2_k + ["lnt3"], writes=["lnt3"])
        sc.dve(lambda e: e.tensor_scalar(out=lnt[3][:], in0=lnt[3][:], scalar1=EPS, scalar2=None, op0=ALU.add),
               reads=["lnt3"], writes=["lnt3"])
        sc.act(lambda e: e.activation(out=lnt[3][:], in_=lnt[3][:], func=AF.Sqrt), reads=["lnt3"], writes=["lnt3"])
        sc.dve(lambda e: e.reciprocal(out=lnt[4][:], in_=lnt[3][:]), reads=["lnt3"], writes=["lnt4"])
        chk('frstd')
        for cc in range(4):
            sc.dve(lambda e, cc=cc: e.tensor_tensor(out=lnt[cc % 2][:], in0=cacc[:, cc, :], in1=lnt[2][:], op=ALU.subtract),
                   reads=[f"cacc{cc}", "lnt2"], writes=[f"lnt{cc % 2}"])
            sc.dve(lambda e, cc=cc: e.tensor_tensor(out=lnt[cc % 2][:], in0=lnt[cc % 2][:], in1=lnt[4][:], op=ALU.mult),
                   reads=[f"lnt{cc % 2}", "lnt4"], writes=[f"lnt{cc % 2}"])
            sc.act(lambda e, cc=cc: e.activation(out=mab[cc % 2][:], in_=lnt[cc % 2][:], func=AF.Silu,
                                                 bias=lnb[:, cc:cc + 1], scale=lng[:, cc:cc + 1]),
                   reads=[f"lnt{cc % 2}", "lng", "lnb"], writes=[f"mab{cc % 2}"])
            sc.dve(lambda e, cc=cc: e.tensor_tensor(out=gatedbT[:, cc, :], in0=mab[cc % 2][:], in1=szbT[:, cc, :], op=ALU.mult),
                   reads=[f"mab{cc % 2}", "szbT"], writes=["gatedbT"])

        dbg('cacc', lambda: cacc[:].rearrange('p k t -> p (k t)'), [f'cacc{c_}' for c_ in range(4)], 1024)
        dbg('gatedbT', lambda: gatedbT[:].rearrange('p k t -> p (k t)'), ['gatedbT'], 1024)
        dbg('lnt2', lambda: lnt[2][:], ['lnt2'], G)
        dbg('lnt4', lambda: lnt[4][:], ['lnt4'], G)
        yield
        for (gname, srcT, skey, wq_x) in (("gta", gatedT, "gatedT", wq_a), ("gtb", gatedbT, "gatedbT", wq_b)):
            wvb = wbr[:].rearrange("p (k c) -> p k c", k=4)
            wkb = "wbr"
            for hf in range(2):
                sc.dma("sp", lambda e, hf=hf, wq_x=wq_x: e.dma_start(out=wbr[:], in_=wq_x[hf]),
                       reads=["wq"], writes=["wbr"])
                bi = WBLOCKS.index(f"{gname}{hf}")
                wtg, wkg = load_w(wq_in[bi])
                wvg = v8(wtg)
                for cc in range(4):
                    dm = hf * 4 + cc
                    gt_, gk_ = ps_half()
                    for kc in range(KC):
                        sc.pe(lambda e, kc=kc, cc=cc, gt_=gt_, wvg=wvg: e.matmul(
                            gt_, lhsT=wvg[:, kc, cc * 128:(cc + 1) * 128], rhs=hT[:, kc, :],
                            start=(kc == 0), stop=(kc == KC - 1)), reads=[wkg, hTk], writes=gk_)
                    gb_ = gtmp[dm % 2]
                    gbk = f"gtmp{dm % 2}"
                    sc.act(lambda e, gt_=gt_, gb_=gb_: e.activation(out=gb_[:], in_=gt_, func=AF.Sigmoid),
                           reads=gk_, writes=[gbk])
                    yt_, yk_ = ps_half()
                    for kc in range(4):
                        sc.pe(lambda e, kc=kc, cc=cc, yt_=yt_, wvb=wvb, srcT=srcT: e.matmul(
                            yt_, lhsT=wvb[:, kc, cc * 128:(cc + 1) * 128], rhs=srcT[:, kc, :],
                            start=(kc == 0), stop=(kc == 3)), reads=[wkb, skey], writes=yk_)
                    if gname == "gta":
                        sc.dve(lambda e, dm=dm, yt_=yt_, gb_=gb_: e.tensor_tensor(
                            out=mergedT[:, dm, :], in0=yt_, in1=gb_[:], op=ALU.mult),
                            reads=yk_ + [gbk], writes=[f"mg{dm}"])
                    else:
                        mb_ = mab[dm % 2]
                        mk_ = f"mab{dm % 2}"
                        sc.dve(lambda e, yt_=yt_, gb_=gb_, mb_=mb_: e.tensor_tensor(
                            out=mb_[:], in0=yt_, in1=gb_[:], op=ALU.mult), reads=yk_ + [gbk], writes=[mk_])
                        sc.dve(lambda e, dm=dm, mb_=mb_: e.tensor_tensor(
                            out=mergedT[:, dm, :], in0=mergedT[:, dm, :], in1=mb_[:], op=ALU.add),
                            reads=[mk_, f"mg{dm}"], writes=[f"mg{dm}"])
                    yield
        sc.dma("pool", lambda e, t0=t0: e.dma_start(out=xs, in_=x_d[t0:t0 + G, :].rearrange("(s p) d -> p s d", p=128)),
               writes=bigkeys(0, SUB * D))
        mgk = [f"mg{dm}" for dm in range(KC)]
        for hf in range(2):
            wt, wk = load_w(wq_out[hf])
            wv = v8(wt)
            for s in range(SUB):
                pt_, pk = ps_full("a")
                for kc in range(KC):
                    sc.pe(lambda e, kc=kc, s=s, pt_=pt_, wv=wv: e.matmul(
                        pt_[:], lhsT=mergedT[:, kc, s * 128:(s + 1) * 128], rhs=wv[:, kc, :],
                        start=(kc == 0), stop=(kc == KC - 1)), reads=[wk] + mgk, writes=pk)
                xk = bigkeys(s * D + hf * 512, s * D + hf * 512 + 512)
                sc.dve(lambda e, s=s, hf=hf, pt_=pt_: e.tensor_tensor(
                    out=xs[:, s, hf * 512:(hf + 1) * 512], in0=xs[:, s, hf * 512:(hf + 1) * 512], in1=pt_[:], op=ALU.add),
                    reads=pk + xk, writes=xk)

        dbg('mergedT', lambda: mergedT[:, 0:4, :].rearrange('p k t -> p (k t)'), [f'mg{d_}' for d_ in range(8)], 1024)
        yield "B"
        sc.act(lambda e: e.activation(out=pbf[:], in_=ptl[:], func=AF.Identity), reads=["ptl"], writes=["pbf"])
        for s in range(SUB):
            pt_, pk = ps_half()
            ptb = pt_.bitcast(BF16)
            for k2 in range(2):
                sc.pe(lambda e, k2=k2, s=s, ptb=ptb: e.transpose(out=ptb[:, k2 * 128:(k2 + 1) * 128],
                                                               in_=pbf[:, s, k2 * 128:(k2 + 1) * 128], identity=identb[:]),
                      reads=["pbf", "identb"], writes=pk)
            copy_any(pT[:, :, s * 128:(s + 1) * 128], ptb[:, 0:256].rearrange("p (k t) -> p k t", k=2), pk, ["pT"])
        yield
        rk = rms_rstd(2)
        yield
        norm_transpose(rk, 2, gple, "gple", hT, hTk)
        yield
        sc.dma("sp", lambda e: e.dma_start(out=wbr[:], in_=wq_proj), reads=["wq"], writes=["wbr"])
        wkp = "wbr"
        wvp = wbr[:].rearrange("p (k c) -> p k c", k=2)
        for hf in range(2):
            wt, wk = load_w(wq_gate[hf])
            wv = v8(wt)
            for s in range(SUB):
                pt_, pk = ps_full("a")
                for kc in range(KC):
                    sc.pe(lambda e, kc=kc, s=s, pt_=pt_, wv=wv: e.matmul(
                        pt_[:], lhsT=hT[:, kc, s * 128:(s + 1) * 128], rhs=wv[:, kc, :],
                        start=(kc == 0), stop=(kc == KC - 1)), reads=[wk, hTk], writes=pk)
                sc.act(lambda e, pt_=pt_: e.activation(out=gsb, in_=pt_[:], func=AF.Sigmoid), reads=pk, writes=["big4"])
                et_, ek_ = ps_full("a")
                for k2 in range(2):
                    sc.pe(lambda e, k2=k2, s=s, hf=hf, et_=et_, wvp=wvp: e.matmul(
                        et_[:], lhsT=pT[:, k2, s * 128:(s + 1) * 128], rhs=wvp[:, k2, hf * 512:(hf + 1) * 512],
                        start=(k2 == 0), stop=(k2 == 1)), reads=[wkp, "pT"], writes=ek_)
                sc.dve(lambda e, et_=et_: e.tensor_tensor(out=tmpe, in0=et_[:], in1=gsb, op=ALU.mult),
                       reads=ek_ + ["big4"], writes=["big5"])
                xk = bigkeys(s * D + hf * 512, s * D + hf * 512 + 512)
                sc.dve(lambda e, s=s, hf=hf: e.tensor_tensor(
                    out=xs[:, s, hf * 512:(hf + 1) * 512], in0=xs[:, s, hf * 512:(hf + 1) * 512], in1=tmpe, op=ALU.add),
                    reads=["big5"] + xk, writes=xk)
                yield

        rk = rms_rstd(4)
        yield
        for s in range(SUB):
            sc.dve(lambda e, s=s: e.scalar_tensor_tensor(out=ob, in0=xs[:, s, :], scalar=stat[:, 4 + s:5 + s], in1=gfin[:],
                                                        op0=ALU.mult, op1=ALU.mult),
                   reads=bigkeys(s * D, (s + 1) * D) + [rk, "gfin"], writes=["big6", "big7"])
            sc.dma("sp", lambda e, s=s, t0=t0: e.dma_start(out=out_d[t0 + s * 128:t0 + (s + 1) * 128, :], in_=ob),
                   reads=["big6", "big7"])
            yield

        yield

    def _drain(gen, n=None):
        k = 0
        for _ in gen:
            k += 1
            if n is not None and k >= n:
                return False
        return True

    _drain(head(0))
    for g in range(NG):
        middle(g)
        tg = tail(g)
        hg = head(g + 1) if g + 1 < NG else iter(())
        nA = 0
        hx = 0
        for v in tg:
            if v == "B":
                break
            nA += 1
            if nA in (5, 9) and hx < SUB:
                next(hg, None)
                hx += 1
        while hx < SUB:
            next(hg, None)
            hx += 1
        td = hd = False
        _end = object()
        while not (td and hd):
            if not hd:
                hd = next(hg, _end) is _end
            if not td:
                td = next(tg, _end) is _end

    info = sc.emit(sems, dsems)
    st.close()
    return nc, info


def _t5_bucket(rel):
    half = 16
    max_exact = 8
    base = np.where(rel > 0, half, 0).astype(np.int32)
    n = np.abs(rel)
    nf = np.maximum(n, 1).astype(np.float32)
    large = max_exact + (np.log(nf / np.float32(max_exact)) / np.float32(math.log(128 / 8))
                         * np.float32(half - max_exact)).astype(np.int32)
    large = np.minimum(large, half - 1)
    return base + np.where(n < max_exact, n, large)


def host_consts(inputs):
    f = lambda a: np.ascontiguousarray(np.asarray(a, dtype=np.float32))
    rel_bias = f(inputs["rel_bias"])
    s_idx = np.arange(128)[:, None]
    t_idx = np.arange(128)[None, :]
    b0 = _t5_bucket((s_idx - t_idx).astype(np.int32))
    b1 = _t5_bucket((s_idx - 128 - t_idx).astype(np.int32))
    d0t = rel_bias[b0]
    d1t = rel_bias[b1]
    c = dict(
        w_in=f(inputs["w_in"][0]), w_a=f(inputs["w_branch_a"][0]), w_b=f(inputs["w_branch_b"][0]),
        w_out=f(inputs["w_out"][0]), w_gate=f(inputs["w_ple_gate"][0]), w_proj=f(inputs["w_ple_proj"][0]),
        gin=f(np.asarray(inputs["norm_in_g"][0]).reshape(KC, 128).T),
        gple=f(np.asarray(inputs["ple_norm_g"][0]).reshape(KC, 128).T),
        gfin=f(np.broadcast_to(np.asarray(inputs["final_norm_g"])[None, :], (128, D))),
        convw=f(np.asarray(inputs["conv_w"][0, :, 0, :]).T.reshape(4, 128, CONVK).transpose(1, 0, 2)),
        convb=f(np.asarray(inputs["conv_b"][0]).reshape(4, 128).T),
        lng=f(np.asarray(inputs["conv_ln_g"][0]).reshape(4, 128).T),
        lnb=f(np.asarray(inputs["conv_ln_b"][0]).reshape(4, 128).T),
        d0t=f(d0t.transpose(0, 2, 1)), d1t=f(d1t.transpose(0, 2, 1)),
        b15=f(np.broadcast_to(rel_bias[15][None, :], (128, 8))),
        ident=np.eye(128, dtype=np.float32),
        pow2=f(np.broadcast_to((2.0 ** -np.arange(NIT + 2, dtype=np.float64))[None, :], (128, NIT + 2))),
    )
    return c


_CACHE = {}


def kernel(**inputs):
    x = np.asarray(inputs["x"], dtype=np.float32)
    p = np.asarray(inputs["p"], dtype=np.float32)
    B, S, _ = x.shape
    if S not in _CACHE:
        _CACHE[S] = build(S)[0]
    nc = _CACHE[S]
    c = host_consts(inputs)
    in_maps = []
    for b in range(B):
        m = dict(c)
        m["x"] = np.ascontiguousarray(x[b])
        m["p"] = np.ascontiguousarray(p[0, b])
        in_maps.append(m)
    res = run_bass_kernel_spmd(nc, in_maps, core_ids=list(range(B)))
    return np.stack([np.asarray(r["out"], dtype=np.float32) for r in res.results], axis=0)
```
